# Optimizing a Trainium2 kernel written in Bass

```python
import numpy as np
import jax
import jax.numpy as jnp
from jax import lax

D_MODEL = 2048
BATCH = 8
SEQ = 2048
DEPTH = 1

CHUNK = 64
N_MEM = 256
NORM_EPS = 1e-6
NEG_INF = -1e30

A_HEADS = 8
A_HEAD_DIM = 128
A_WIDTH = A_HEADS * A_HEAD_DIM
A_LEFT_CHUNKS = 8
A_BAND = (A_LEFT_CHUNKS + 1) * CHUNK
REL_CLIP = 128

B_HEADS = 16
B_HEAD_DIM = 64
B_WIDTH = B_HEADS * B_HEAD_DIM
DECAY_LORA = 64
ICLR_LORA = 64
GN_EPS = 64e-5

C_HEADS = 4
C_HEAD_DIM = 256
C_WIDTH = C_HEADS * C_HEAD_DIM

IN_SPLITS = (A_WIDTH,) * 4 + (B_WIDTH,) * 4 + (DECAY_LORA, ICLR_LORA) + (C_WIDTH,) * 2 + (D_MODEL,) * 3
IN_COLS = sum(IN_SPLITS)

kernel_name = "hybrid_chunkattn_rwkv7_memxattn"


def rms_norm(x, g):
    xf = x.astype(jnp.float32)
    y = xf * lax.rsqrt(jnp.mean(xf * xf, axis=-1, keepdims=True) + NORM_EPS)
    return (y * g.astype(jnp.float32)).astype(x.dtype)


def _rel_index():
    r = np.arange(CHUNK)[:, None]
    m = np.arange(A_BAND)[None, :]
    dist = r + A_LEFT_CHUNKS * CHUNK - m
    return np.clip(dist, -REL_CLIP, REL_CLIP) + REL_CLIP


def chunked_band_attention(q, k, v, rel_bias):
    bsz, s, h, dh = q.shape
    n_chunks = s // CHUNK
    pad = A_LEFT_CHUNKS * CHUNK
    kp = jnp.pad(k, ((0, 0), (pad, 0), (0, 0), (0, 0)))
    vp = jnp.pad(v, ((0, 0), (pad, 0), (0, 0), (0, 0)))
    bias = rel_bias[:, _rel_index()].astype(jnp.float32)
    band_pos = jnp.arange(A_BAND) - pad
    scale = dh ** -0.5

    def one_chunk(c):
        start = c * CHUNK
        qc = lax.dynamic_slice_in_dim(q, start, CHUNK, axis=1)
        kc = lax.dynamic_slice_in_dim(kp, start, A_BAND, axis=1)
        vc = lax.dynamic_slice_in_dim(vp, start, A_BAND, axis=1)
        sc = jnp.einsum('bqhd,bkhd->bhqk', qc, kc, preferred_element_type=jnp.float32) * scale + bias
        valid = (start + band_pos) >= 0
        sc = jnp.where(valid[None, None, None, :], sc, NEG_INF)
        p = jax.nn.softmax(sc, axis=-1)
        return jnp.einsum('bhqk,bkhd->bqhd', p.astype(vc.dtype), vc)

    out = lax.map(one_chunk, jnp.arange(n_chunks))
    return jnp.transpose(out, (1, 0, 2, 3, 4)).reshape(bsz, s, h * dh)


def token_shift_lerp(p, mu):
    prev = jnp.pad(p, ((0, 0), (1, 0), (0, 0)))[:, :-1]
    return p + mu * (prev - p)


def rwkv7_scan(r, decay, k, v, a_vec, b_vec):
    bsz, _, h, n = r.shape

    def step(state, inp):
        r_t, w_t, k_t, v_t, a_t, b_t = inp
        sa = jnp.einsum('bhij,bhj->bhi', state, a_t)
        state = state * w_t[:, :, None, :] + sa[..., None] * b_t[:, :, None, :] + v_t[..., None] * k_t[:, :, None, :]
        return state, jnp.einsum('bhij,bhj->bhi', state, r_t)

    xs = tuple(jnp.moveaxis(t, 1, 0) for t in (r, decay, k, v, a_vec, b_vec))
    s0 = jnp.zeros((bsz, h, n, n), jnp.float32)
    _, out = lax.scan(step, s0, xs)
    return jnp.moveaxis(out, 0, 1)


def rwkv7_time_mix(p_r, p_k, p_v, p_wd, p_ad, mu_rkv, mu_w, mu_a, w0, w2, a0, a2, k_k, k_a, r_k, ln_w, ln_b):
    f = lambda t: t.astype(jnp.float32)
    p_r, p_k, p_v, p_wd, p_ad = f(p_r), f(p_k), f(p_v), f(p_wd), f(p_ad)
    bsz, s, _ = p_r.shape
    r = token_shift_lerp(p_r, f(mu_rkv[0]))
    k = token_shift_lerp(p_k, f(mu_rkv[1]))
    v = token_shift_lerp(p_v, f(mu_rkv[2]))
    wd = token_shift_lerp(p_wd, f(mu_w))
    ad = token_shift_lerp(p_ad, f(mu_a))
    w = -jax.nn.softplus(-(f(w0) + jnp.tanh(wd) @ f(w2))) - 0.5
    decay = jnp.exp(-jnp.exp(w))
    a = jax.nn.sigmoid(f(a0) + ad @ f(a2))
    heads = lambda t: t.reshape(bsz, s, B_HEADS, B_HEAD_DIM)
    kk = heads(k * f(k_k))
    kk = kk * lax.rsqrt(jnp.maximum(jnp.sum(kk * kk, axis=-1, keepdims=True), 1e-24))
    k = k * (1.0 + (a - 1.0) * f(k_a))
    rh, kh, vh, ah, dh = heads(r), heads(k), heads(v), heads(a), heads(decay)
    o = rwkv7_scan(rh, dh, kh, vh, -kk, kk * ah)
    mean = jnp.mean(o, axis=-1, keepdims=True)
    var = jnp.mean(jnp.square(o - mean), axis=-1, keepdims=True)
    o = ((o - mean) * lax.rsqrt(var + GN_EPS)).reshape(bsz, s, B_WIDTH) * f(ln_w) + f(ln_b)
    bonus = jnp.sum(rh * kh * f(r_k), axis=-1, keepdims=True) * vh
    return o + bonus.reshape(bsz, s, B_WIDTH)


def memory_cross_attention(q, mk, mv):
    bsz, s, h, dh = q.shape
    sc = jnp.einsum('bshd,bmhd->bhsm', q, mk, preferred_element_type=jnp.float32) * (dh ** -0.5)
    p = jax.nn.softmax(sc, axis=-1)
    out = jnp.einsum('bhsm,bmhd->bshd', p.astype(mv.dtype), mv)
    return out.reshape(bsz, s, h * dh)


def hybrid_layer(x, mem, norm_g, w_in, a_q_g, a_k_g, a_rel_bias, w_up_a,
                 b_mu_rkv, b_mu_w, b_mu_a, b_w0, b_w2, b_a0, b_a2, b_k_k, b_k_a, b_r_k,
                 b_ln_w, b_ln_b, w_up_b, mem_norm_g, w_mem_kv, c_q_g, c_k_g, w_up_c, w_o):
    bsz, s, _ = x.shape
    h = rms_norm(x, norm_g)
    proj = h @ w_in
    offsets = np.cumsum(np.array(IN_SPLITS))[:-1].tolist()
    (aq, ak, av, az, br, bk, bv, bz, bwd, bad, cq, cz, ga, gb, gc) = jnp.split(proj, offsets, axis=-1)

    aq = rms_norm(aq.reshape(bsz, s, A_HEADS, A_HEAD_DIM), a_q_g)
    ak = rms_norm(ak.reshape(bsz, s, A_HEADS, A_HEAD_DIM), a_k_g)
    av = av.reshape(bsz, s, A_HEADS, A_HEAD_DIM)
    ya = chunked_band_attention(aq, ak, av, a_rel_bias) * jax.nn.silu(az)

    yb = rwkv7_time_mix(br, bk, bv, bwd, bad, b_mu_rkv, b_mu_w, b_mu_a, b_w0, b_w2, b_a0, b_a2,
                        b_k_k, b_k_a, b_r_k, b_ln_w, b_ln_b).astype(x.dtype) * jax.nn.silu(bz)

    m = rms_norm(mem, mem_norm_g)
    mk, mv = jnp.split(m @ w_mem_kv, 2, axis=-1)
    mk = rms_norm(mk.reshape(bsz, N_MEM, C_HEADS, C_HEAD_DIM), c_k_g)
    mv = mv.reshape(bsz, N_MEM, C_HEADS, C_HEAD_DIM)
    cq = rms_norm(cq.reshape(bsz, s, C_HEADS, C_HEAD_DIM), c_q_g)
    yc = memory_cross_attention(cq, mk, mv) * jax.nn.silu(cz)

    merged = (jax.nn.sigmoid(ga) * (ya @ w_up_a)
              + jax.nn.sigmoid(gb) * (yb @ w_up_b)
              + jax.nn.sigmoid(gc) * (yc @ w_up_c))
    return x + merged @ w_o


def setup_inputs(seed: int = 0) -> dict:
    key = jax.random.key(seed)
    ks = jax.random.split(key, 32)
    nrm = lambda k, shape, sc: jax.random.normal(k, shape, jnp.float32) * sc
    L = DEPTH
    return {
        "x": nrm(ks[0], (BATCH, SEQ, D_MODEL), 1.0),
        "mem": nrm(ks[1], (BATCH, N_MEM, D_MODEL), 1.0),
        "norm_g": 1.0 + nrm(ks[2], (L, D_MODEL), 0.1),
        "w_in": nrm(ks[3], (L, D_MODEL, IN_COLS), D_MODEL ** -0.5),
        "a_q_g": 1.0 + nrm(ks[4], (L, A_HEAD_DIM), 0.1),
        "a_k_g": 1.0 + nrm(ks[5], (L, A_HEAD_DIM), 0.1),
        "a_rel_bias": nrm(ks[6], (L, A_HEADS, 2 * REL_CLIP + 1), 0.5),
        "w_up_a": nrm(ks[7], (L, A_WIDTH, D_MODEL), A_WIDTH ** -0.5),
        "b_mu_rkv": jax.random.uniform(ks[8], (L, 3, B_WIDTH), jnp.float32),
        "b_mu_w": jax.random.uniform(ks[9], (L, DECAY_LORA), jnp.float32),
        "b_mu_a": jax.random.uniform(ks[10], (L, ICLR_LORA), jnp.float32),
        "b_w0": jax.random.uniform(ks[11], (L, B_WIDTH), jnp.float32, minval=-3.0, maxval=0.5),
        "b_w2": nrm(ks[12], (L, DECAY_LORA, B_WIDTH), 0.1),
        "b_a0": nrm(ks[13], (L, B_WIDTH), 0.5),
        "b_a2": nrm(ks[14], (L, ICLR_LORA, B_WIDTH), 0.1),
        "b_k_k": 0.85 + nrm(ks[15], (L, B_WIDTH), 0.05),
        "b_k_a": 1.0 + nrm(ks[16], (L, B_WIDTH), 0.05),
        "b_r_k": nrm(ks[17], (L, B_HEADS, B_HEAD_DIM), 0.1),
        "b_ln_w": 1.0 + nrm(ks[18], (L, B_WIDTH), 0.1),
        "b_ln_b": nrm(ks[19], (L, B_WIDTH), 0.02),
        "w_up_b": nrm(ks[20], (L, B_WIDTH, D_MODEL), B_WIDTH ** -0.5),
        "mem_norm_g": 1.0 + nrm(ks[21], (L, D_MODEL), 0.1),
        "w_mem_kv": nrm(ks[22], (L, D_MODEL, 2 * C_WIDTH), D_MODEL ** -0.5),
        "c_q_g": 1.0 + nrm(ks[23], (L, C_HEAD_DIM), 0.1),
        "c_k_g": 1.0 + nrm(ks[24], (L, C_HEAD_DIM), 0.1),
        "w_up_c": nrm(ks[25], (L, C_WIDTH, D_MODEL), C_WIDTH ** -0.5),
        "w_o": nrm(ks[26], (L, D_MODEL, D_MODEL), D_MODEL ** -0.5),
    }


def reference(x, mem, norm_g, w_in, a_q_g, a_k_g, a_rel_bias, w_up_a,
              b_mu_rkv, b_mu_w, b_mu_a, b_w0, b_w2, b_a0, b_a2, b_k_k, b_k_a, b_r_k,
              b_ln_w, b_ln_b, w_up_b, mem_norm_g, w_mem_kv, c_q_g, c_k_g, w_up_c, w_o):
    for l in range(DEPTH):
        x = hybrid_layer(x, mem, norm_g[l], w_in[l], a_q_g[l], a_k_g[l], a_rel_bias[l], w_up_a[l],
                         b_mu_rkv[l], b_mu_w[l], b_mu_a[l], b_w0[l], b_w2[l], b_a0[l], b_a2[l],
                         b_k_k[l], b_k_a[l], b_r_k[l], b_ln_w[l], b_ln_b[l], w_up_b[l],
                         mem_norm_g[l], w_mem_kv[l], c_q_g[l], c_k_g[l], w_up_c[l], w_o[l])
    return x
```

```python
import contextlib
import math
import numpy as np
import concourse.bass as bass
import concourse.mybir as mybir
from concourse.bass_utils import run_bass_kernel_spmd

F32 = mybir.dt.float32
BF16 = mybir.dt.bfloat16
AF = mybir.ActivationFunctionType
ALU = mybir.AluOpType
AX = mybir.AxisListType

D = 2048
T = 2048
TH = 512
NPASS = T // TH
NT = TH // 128
KC = D // 128
NMEM = 256
IN_COLS = 16512
OFF_AQ, OFF_AK, OFF_AV, OFF_AZ = 0, 1024, 2048, 3072
OFF_BR, OFF_BK, OFF_BV, OFF_BZ = 4096, 5120, 6144, 7168
OFF_WD, OFF_AD = 8192, 8256
OFF_CQ, OFF_CZ = 8320, 9344
OFF_GA, OFF_GB, OFF_GC = 10368, 12416, 14464
NORM_EPS = 1e-6
GN_EPS = 64e-5
CH = 64
NCH = TH // CH
SEM_LIMIT = 50000
SAME_ENG_SYNC = True
SAME_ENG_WAR = False
INTERLEAVE_AC = False
CH_BF = True
WS = 4096
NWS = 4

PC_G = 0
PC_MG = 16
PC_AQG = 32
PC_AKG = 33
PC_CQG = 34
PC_CKG = 36
PC_MUWA = 38
PC_B = 40
(PB_MUR, PB_MUK, PB_MUV, PB_W0, PB_A0, PB_KK, PB_KA, PB_RK, PB_LNW, PB_LNB) = range(10)
NPARAM = PC_B + 80


class Sem:
    _n = 0

    def __init__(self, h):
        self.h = h
        Sem._n += 1
        self.id = Sem._n


class Region:
    __slots__ = ("w", "r", "dsem", "dcount")

    def __init__(self):
        self.w = None
        self.r = {}
        self.dsem = None
        self.dcount = 0


class Buf:
    def __init__(self, t, regs):
        self.t = t
        self.regs = regs

    def __getitem__(self, k):
        return self.t[k]


class Engine:
    def __init__(self, S, name, eng, counting=True):
        self.S = S
        self.name = name
        self.eng = eng
        self.counting = counting
        self.seen = {}
        self.sem = None
        self.count = 0
        self.pending = False
        self.nins = 0
        if counting:
            self.sem = S.new_sem(name)


class Sched:
    def __init__(self, nc, es):
        self.nc = nc
        self.es = es
        self.nsem = 0
        self.PE = Engine(self, "pe", nc.tensor)
        self.ACT = Engine(self, "act", nc.scalar)
        self.DVE = Engine(self, "dve", nc.vector)
        self.POOL = Engine(self, "pool", nc.gpsimd)
        self.SP = Engine(self, "sp", nc.sync, counting=False)
        self.final = []
        self.sbuf_bytes = 0

    def new_sem(self, name):
        self.nsem += 1
        return Sem(self.es.enter_context(self.nc.semaphore("s%d_%s" % (self.nsem, name))))

    def sbuf(self, name, shape, dtype, nregs=1):
        t = self.es.enter_context(self.nc.sbuf_tensor("sb_" + name, list(shape), dtype))
        n = 1
        for s in shape[1:]:
            n *= s
        self.sbuf_bytes += n * (4 if dtype == F32 else 2)
        return Buf(t, [Region() for _ in range(nregs)])

    def psum(self, name, shape, dtype):
        t = self.es.enter_context(self.nc.psum_tensor("ps_" + name, list(shape), dtype))
        return Buf(t, [Region()])

    def _waits(self, E, reads, writes):
        need = {}

        def add(tok, war=False):
            if tok is None:
                return
            sem, val, en = tok
            if en == E.name and (E.name == "pe" or not SAME_ENG_SYNC or (war and not SAME_ENG_WAR)):
                return
            if E.seen.get(sem.id, 0) >= val:
                return
            if sem.id not in need or need[sem.id][1] < val:
                need[sem.id] = (sem, val)

        for b in reads:
            for rg in b.regs:
                add(rg.w)
        for b in writes:
            for rg in b.regs:
                add(rg.w)
                for tok in rg.r.values():
                    add(tok, war=True)
        for sem, val in need.values():
            E.eng.wait_ge(sem.h, val)
            E.seen[sem.id] = val
            E.nins += 1

    def op(self, E, fn, reads=(), writes=(), inc=True):
        self._waits(E, reads, writes)
        if inc and E.count >= SEM_LIMIT and not E.pending:
            E.sem = self.new_sem(E.name)
            E.count = 0
        ins = fn()
        E.nins += 1
        val = E.count + 1
        tok = (E.sem, val, E.name)
        if inc:
            ins.then_inc(E.sem.h, 1)
            E.count = val
            E.pending = False
        else:
            E.pending = True
        for b in writes:
            for rg in b.regs:
                rg.w = tok
                rg.r = {}
        for b in reads:
            for rg in b.regs:
                if rg.w is not tok:
                    rg.r[E.name] = tok
        return ins

    def dma(self, Q, out_ap, in_ap, reads=(), writes=(), final=False):
        self._waits(Q, reads, writes)
        ins = Q.eng.dma_start(out=out_ap, in_=in_ap)
        Q.nins += 1
        rg0 = (list(writes) + list(reads))[0].regs[0]
        if rg0.dsem is None or rg0.dcount + 16 > SEM_LIMIT:
            rg0.dsem = self.new_sem("d")
            rg0.dcount = 0
        rg0.dcount += 16
        ins.then_inc(rg0.dsem.h, 16)
        tok = (rg0.dsem, rg0.dcount, "dma")
        for b in writes:
            for rg in b.regs:
                rg.w = tok
                rg.r = {}
        for b in reads:
            for rg in b.regs:
                rg.r["dma%d" % rg0.dsem.id] = tok
        if final:
            self.final.append(tok)
        return ins

    def finish(self):
        need = {}
        for sem, val, _ in self.final:
            if sem.id not in need or need[sem.id][1] < val:
                need[sem.id] = (sem, val)
        for sem, val in need.values():
            self.SP.eng.wait_ge(sem.h, val)


class Pool:
    def __init__(self, S, name, n, w, dtype):
        self.S = S
        self.n = n
        self.w = w
        self.buf = S.sbuf(name, [128, n, w], dtype, nregs=n)
        self.free = list(range(n))
        self.name = name

    def get(self, k=1):
        fs = sorted(self.free)
        for i in fs:
            if all((i + j) in self.free for j in range(k)):
                for j in range(k):
                    self.free.remove(i + j)
                regs = self.buf.regs[i:i + k]
                if k == 1:
                    b = Buf(self.buf.t[:, i, :], regs)
                else:
                    b = Buf(self.buf.t[:, i:i + k, :].rearrange("p k n -> p (k n)"), regs)
                b.slot = (i, k)
                return b
        raise RuntimeError("pool %s exhausted (free=%s, want %d)" % (self.name, self.free, k))

    def put(self, *bs):
        for b in bs:
            i, k = b.slot
            for j in range(k):
                assert (i + j) not in self.free
                self.free.append(i + j)


def build_program(dbg=None, phases="0ACBMO", npass=NPASS):
    dbg = dbg or {}
    nc = bass.Bass("TRN2", target_bir_lowering=False)
    es = contextlib.ExitStack()
    S = Sched(nc, es)
    PE, ACT, DVE, POOL, SP = S.PE, S.ACT, S.DVE, S.POOL, S.SP
    te, se, ve, ge = nc.tensor, nc.scalar, nc.vector, nc.gpsimd

    def din(name, shape, dtype=F32):
        return nc.dram_tensor(name, list(shape), dtype, kind="ExternalInput").ap()

    x_d = din("x", [T, D])
    mem_d = din("mem", [NMEM, D])
    w_in = din("w_in", [D, IN_COLS])
    w_up_a = din("w_up_a", [1024, D])
    w_up_b = din("w_up_b", [1024, D])
    w_up_c = din("w_up_c", [1024, D])
    w_memkv = din("w_mem_kv", [D, 2048])
    w_o = din("w_o", [D, D])
    params_d = din("params", [128, NPARAM])
    w2a2_d = din("w2a2", [128, 1024])
    biasT_d = din("biasT", [8, 128, 640])
    consts_d = din("consts", [128, 832])
    cmask_d = din("cmask", [128, TH])
    out_d = nc.dram_tensor("out", [T, D], F32, kind="ExternalOutput").ap()
    dbg_out = {}

    hT = S.sbuf("hT", [128, KC, TH], BF16)
    yaT = S.sbuf("yaT", [128, 8, TH], BF16)
    ybT = S.sbuf("ybT", [128, 8, TH], BF16)
    ycT = S.sbuf("ycT", [128, 8, TH], BF16)
    mgT = S.sbuf("mgT", [128, KC, TH], BF16, nregs=KC)
    wsl = [S.sbuf("ws%d" % i, [128, WS], BF16, nregs=8) for i in range(NWS)]
    kT = S.sbuf("kT", [128, 8, 2 * TH], BF16)
    vS = S.sbuf("vS", [128, 8, 1024], BF16)
    mkT = S.sbuf("mkT", [128, 8, NMEM], BF16)
    mvS = S.sbuf("mvS", [128, 2, 1024], BF16)
    params = S.sbuf("params", [128, NPARAM], F32)
    derived = S.sbuf("derived", [128, 64], F32)
    w2a2 = S.sbuf("w2a2", [128, 1024], BF16)
    consts = S.sbuf("consts", [128, 832], F32)
    identb = S.sbuf("identb", [128, 128], BF16)
    onesb = S.sbuf("onesb", [128, 128], BF16)
    blk1b = S.sbuf("blk1b", [128, 128], BF16)
    cmask = S.sbuf("cmask", [128, TH], BF16)
    STs = S.sbuf("STs", [128, 8, 64], F32)
    carry = S.sbuf("carry", [128, 8, 4], F32)
    carry_wa = S.sbuf("carry_wa", [128, 2], F32)
    biasm = S.sbuf("biasm", [128, 640], F32)
    smalls = S.sbuf("smalls", [128, 64, 8], F32, nregs=64)
    CDT = BF16 if CH_BF else F32
    F4 = Pool(S, "F4", 12 if CH_BF else 14, 512, F32)
    BFp = Pool(S, "BFp", 20 if CH_BF else 6, 512, BF16)
    CPool = BFp if CH_BF else F4
    STb = S.sbuf("STb", [128, 8, 64], CDT) if CH_BF else None
    if CH_BF:
        NB = [Buf(mgT.t[0:64, 12 + 2 * i:14 + 2 * i, :].rearrange("q a b -> q (a b)").rearrange("q (m n) -> q m n", n=128),
                  mgT.regs[12 + 2 * i:14 + 2 * i]) for i in range(2)]
    else:
        NB = [S.sbuf("NB%d" % i, [64, 8, 128], CDT) for i in range(2)]
    TM = S.sbuf("TMm", [64, 8, 64], CDT)
    AM3 = S.sbuf("AM3", [64, 8, 192], CDT)
    TMs_ = [TM, S.sbuf("TMm2", [64, 8, 64], CDT)]
    AM3s_ = [AM3, S.sbuf("AM3b", [64, 8, 192], CDT)]
    if CH_BF:
        TMc = Buf(mgT.t[0:64, 0:6, :].rearrange("q a b -> q (a b)").rearrange("q (c n) -> q c n", n=384), mgT.regs[0:6])
        TMc2 = Buf(mgT.t[0:64, 6:12, :].rearrange("q a b -> q (a b)").rearrange("q (c n) -> q c n", n=384), mgT.regs[6:12])
    else:
        TMc = S.sbuf("TMc", [64, 8, 384], CDT)
        TMc2 = S.sbuf("TMc2", [64, 8, 384], CDT)
    PCbufs = [S.sbuf("PCb%d" % i, [128, 8], F32) for i in range(2)]
    OT = S.sbuf("OTb", [64, 8, 128], F32)
    gnst = S.sbuf("gnst", [64, 64], F32)
    psF = [S.psum("psF%d" % i, [128, 512], F32) for i in range(4)]
    psAcc = [S.psum("psAcc%d" % i, [128, 512], F32) for i in range(2)]
    psB = [S.psum("psB%d" % i, [128, 1024], BF16) for i in range(2)]
    rr = {"f": 0, "b": 0, "s": 0, "w": 0}

    def nextF():
        rr["f"] = (rr["f"] + 1) % len(psF)
        return psF[rr["f"]]

    def nextB():
        rr["b"] = (rr["b"] + 1) % len(psB)
        return psB[rr["b"]]

    def small():
        rr["s"] = (rr["s"] + 1) % 64
        i = rr["s"]
        return Buf(smalls.t[:, i, :], [smalls.regs[i]])

    def nextW():
        rr["w"] = (rr["w"] + 1) % NWS
        return wsl[rr["w"]]

    ident = Buf(consts.t[:, 0:128], consts.regs)
    blk1 = Buf(consts.t[:, 128:256], consts.regs)
    rmaskU = Buf(consts.t[0:64, 256:768], consts.regs)
    rmaskL = Buf(consts.t[0:64, 768:832], consts.regs)

    def pcol(c, n=1):
        return params.t[:, c:c + n]

    def dcol(c, n=1):
        return derived.t[:, c:c + n]

    DC_GQS, DC_CQS, DC_OMWA, DC_OMM, DC_OMKA = 0, 1, 3, 4, 28

    S.dma(SP, params.t[:, :], params_d, writes=[params])
    S.dma(SP, consts.t[:, :], consts_d, writes=[consts])
    S.dma(POOL, w2a2.t[:, :], w2a2_d, writes=[w2a2])
    S.op(DVE, lambda: ve.tensor_copy(out=identb.t[:, :], in_=ident[:, :]), reads=[consts], writes=[identb])
    S.op(DVE, lambda: ve.memset(onesb.t[:, :], 1.0), writes=[onesb])
    S.op(DVE, lambda: ve.tensor_copy(out=blk1b.t[:, :], in_=blk1[:, :]), reads=[consts], writes=[blk1b])
    S.dma(POOL, cmask.t[:, :], cmask_d, writes=[cmask])
    S.op(DVE, lambda: ve.memset(STs.t[:, :, :], 0.0), writes=[STs])
    if CH_BF:
        S.op(DVE, lambda: ve.memset(STb.t[:, :, :], 0.0), writes=[STb])
    SR = STb if CH_BF else STs
    S.op(DVE, lambda: ve.memset(carry.t[:, :, :], 0.0), writes=[carry])
    S.op(DVE, lambda: ve.memset(carry_wa.t[:, :], 0.0), writes=[carry_wa])
    S.op(DVE, lambda: ve.tensor_scalar(out=dcol(DC_GQS), in0=pcol(PC_AQG), scalar1=128.0 ** -0.5, scalar2=None,
                                       op0=ALU.mult), reads=[params], writes=[derived])
    S.op(DVE, lambda: ve.tensor_scalar(out=dcol(DC_CQS, 2), in0=pcol(PC_CQG, 2), scalar1=256.0 ** -0.5, scalar2=None,
                                       op0=ALU.mult), reads=[params], writes=[derived])
    S.op(DVE, lambda: ve.tensor_scalar(out=dcol(DC_OMWA), in0=pcol(PC_MUWA), scalar1=-1.0, scalar2=1.0,
                                       op0=ALU.mult, op1=ALU.add), reads=[params], writes=[derived])
    S.op(DVE, lambda: ve.tensor_scalar(out=dcol(DC_OMM, 24), in0=pcol(PC_B + PB_MUR * 8, 24), scalar1=-1.0, scalar2=1.0,
                                       op0=ALU.mult, op1=ALU.add), reads=[params], writes=[derived])
    S.op(DVE, lambda: ve.tensor_scalar(out=dcol(DC_OMKA, 8), in0=pcol(PC_B + PB_KA * 8, 8), scalar1=-1.0, scalar2=1.0,
                                       op0=ALU.mult, op1=ALU.add), reads=[params], writes=[derived])

    def dump(name, ap, shape, reads):
        if name not in dbg:
            return
        d = nc.dram_tensor("dbg_" + name, list(shape), ap.dtype, kind="ExternalOutput").ap()
        dbg_out[name] = d
        S.dma(SP, d, ap, reads=reads, final=True)

    def load_w(src_ap, ncols, kc=KC):
        w = nextW()
        dst = w.t[:, 0:kc * ncols].rearrange("p (k n) -> p k n", n=ncols)
        srcv = src_ap.rearrange("(k p) n -> p k n", p=128)
        for gi in range(kc // 4):
            S.dma(POOL, dst[:, gi * 4:(gi + 1) * 4, :], srcv[:, gi * 4:(gi + 1) * 4, :], writes=[Buf(w.t, [w.regs[gi]])])
        return w

    def rms_to_T(src_rows_ap, gcol, dst, col0):
        xt = F4.get(4)
        hb = BFp.get(4)
        ss = small()
        S.dma(SP, xt[:, :], src_rows_ap, writes=[xt])
        S.op(ACT, lambda: se.activation(out=hb[:, :], in_=xt[:, :], func=AF.Square, accum_out=ss[:, 0:1]),
             reads=[xt], writes=[hb, ss])
        S.op(DVE, lambda: ve.tensor_scalar(out=ss[:, 1:2], in0=ss[:, 0:1], scalar1=1.0 / D, scalar2=NORM_EPS,
                                           op0=ALU.mult, op1=ALU.add), reads=[ss], writes=[ss])
        S.op(ACT, lambda: se.activation(out=ss[:, 2:3], in_=ss[:, 1:2], func=AF.Ln), reads=[ss], writes=[ss])
        S.op(ACT, lambda: se.activation(out=ss[:, 2:3], in_=ss[:, 2:3], func=AF.Exp, scale=-0.5), reads=[ss], writes=[ss])
        S.op(DVE, lambda: ve.tensor_scalar(out=hb[:, :], in0=xt[:, :], scalar1=ss[:, 2:3], scalar2=None,
                                           op0=ALU.mult), reads=[xt, ss], writes=[hb])
        for half in range(2):
            pb = nextB()
            for j in range(8):
                kc = half * 8 + j
                S.op(PE, lambda: te.transpose(out=pb[:, j * 128:(j + 1) * 128], in_=hb[:, kc * 128:(kc + 1) * 128],
                                              identity=identb.t[:, :]),
                     reads=[hb, identb], writes=[pb], inc=(j == 7))
            S.op(DVE, lambda: ve.tensor_tensor(out=dst.t[:, half * 8:(half + 1) * 8, col0:col0 + 128],
                                               in0=pb[:, 0:1024].rearrange("q (k n) -> q k n", n=128),
                                               in1=params.t[:, gcol + half * 8:gcol + half * 8 + 8].unsqueeze(2).to_broadcast([128, 8, 128]),
                                               op=ALU.mult), reads=[pb, params], writes=[dst])
        F4.put(xt)
        BFp.put(hb)

    def proj_T(src, tt, w, ncols, ps, ntok_off=0):
        for kc in range(KC):
            S.op(PE, lambda: te.matmul(ps[:, 0:ncols], lhsT=src.t[:, kc, ntok_off + tt * 128:ntok_off + (tt + 1) * 128],
                                       rhs=w.t[:, kc * ncols:(kc + 1) * ncols], start=(kc == 0), stop=(kc == KC - 1)),
                 reads=[src, w], writes=[ps], inc=(kc == KC - 1))

    def proj_F(src, w, ncols, c0, ps, ntok=TH):
        for kc in range(KC):
            S.op(PE, lambda: te.matmul(ps[:, 0:ntok], lhsT=w.t[:, kc * ncols + c0:kc * ncols + c0 + 128],
                                       rhs=src.t[:, kc, 0:ntok], start=(kc == 0), stop=(kc == KC - 1)),
                 reads=[src, w], writes=[ps], inc=(kc == KC - 1))

    deferred = []

    def flush(keep=0):
        while len(deferred) > keep:
            deferred.pop(0)()

    def qk_norm_T(ps, nh, hd, gcols, dst_fn, eps=NORM_EPS):
        ndc = hd // 128
        ss = small()
        junk = BFp.get()
        for h in range(nh):
            S.op(ACT, lambda: se.activation(out=junk[:, 0:hd], in_=ps[:, h * hd:(h + 1) * hd], func=AF.Square,
                                            accum_out=ss[:, h:h + 1]), reads=[ps], writes=[junk, ss])
        S.op(DVE, lambda: ve.tensor_scalar(out=ss[:, 4:4 + nh], in0=ss[:, 0:nh], scalar1=1.0 / hd, scalar2=eps,
                                           op0=ALU.mult, op1=ALU.add), reads=[ss], writes=[ss])
        S.op(ACT, lambda: se.activation(out=ss[:, 0:nh], in_=ss[:, 4:4 + nh], func=AF.Ln), reads=[ss], writes=[ss])
        S.op(ACT, lambda: se.activation(out=ss[:, 0:nh], in_=ss[:, 0:nh], func=AF.Exp, scale=-0.5), reads=[ss], writes=[ss])
        qn = junk
        for h in range(nh):
            S.op(DVE, lambda: ve.tensor_scalar(out=qn[:, h * hd:(h + 1) * hd], in0=ps[:, h * hd:(h + 1) * hd],
                                               scalar1=ss[:, h:h + 1], scalar2=None, op0=ALU.mult),
                 reads=[ps, ss], writes=[qn])
        nblk = nh * ndc

        def part2():
            pb = nextB()
            for i in range(nblk):
                S.op(PE, lambda: te.transpose(out=pb[:, i * 128:(i + 1) * 128], in_=qn[:, i * 128:(i + 1) * 128],
                                              identity=identb.t[:, :]), reads=[qn, identb], writes=[pb], inc=(i == nblk - 1))
            for i in range(nblk):
                dc = i % ndc
                oap, obuf = dst_fn(i)
                S.op(ACT, lambda: se.mul(out=oap, in_=pb[:, i * 128:(i + 1) * 128], mul=gcols[dc]),
                     reads=[pb, params, derived], writes=[obuf])
            BFp.put(junk)

        deferred.append(part2)
        flush(keep=1)

    def phase_mem():
        for mt in range(2):
            rms_to_T(mem_d[mt * 128:(mt + 1) * 128, :], PC_MG, mgT, mt * 128)
        for blk in range(4):
            w = load_w(w_memkv[:, blk * 256:(blk + 1) * 256], 256)
            for mt in range(2):
                ps = nextF()
                proj_T(mgT, mt, w, 256, ps)
                qk_norm_T(ps, 1, 256, [pcol(PC_CKG), pcol(PC_CKG + 1)],
                          lambda i, blk=blk, mt=mt: (mkT.t[:, blk * 2 + i, mt * 128:(mt + 1) * 128], mkT))
        flush()
        for blk in range(4):
            w = load_w(w_memkv[:, 1024 + blk * 256:1024 + (blk + 1) * 256], 256)
            for mt in range(2):
                ps = nextF()
                proj_T(mgT, mt, w, 256, ps)
                S.op(ACT, lambda: se.copy(out=mvS.t[:, mt, blk * 256:(blk + 1) * 256], in_=ps[:, 0:256]),
                     reads=[ps], writes=[mvS])

    def phase0(ps_i):
        t0 = ps_i * TH
        for tt in range(NT):
            rms_to_T(x_d[t0 + tt * 128:t0 + (tt + 1) * 128, :], PC_G, hT, tt * 128)

    def phaseA(ps_i):
        if ps_i > 0:
            S.op(POOL, lambda: ge.tensor_copy(out=kT.t[:, :, 0:TH], in_=kT.t[:, :, TH:2 * TH]), reads=[kT], writes=[kT])
            S.op(POOL, lambda: ge.tensor_copy(out=vS.t[:, 0:4, :], in_=vS.t[:, 4:8, :]), reads=[vS], writes=[vS])
        for blk in range(4):
            w = load_w(w_in[:, OFF_AK + blk * 256:OFF_AK + (blk + 1) * 256], 256)
            for tt in range(NT):
                ps = nextF()
                proj_T(hT, tt, w, 256, ps)
                qk_norm_T(ps, 2, 128, [pcol(PC_AKG)],
                          lambda i, blk=blk, tt=tt: (kT.t[:, blk * 2 + i, TH + tt * 128:TH + (tt + 1) * 128], kT))
                yield
        flush()
        for blk in range(4):
            w = load_w(w_in[:, OFF_AV + blk * 256:OFF_AV + (blk + 1) * 256], 256)
            for tt in range(NT):
                ps = nextF()
                proj_T(hT, tt, w, 256, ps)
                S.op(ACT, lambda: se.copy(out=vS.t[:, 4 + tt, blk * 256:(blk + 1) * 256], in_=ps[:, 0:256]),
                     reads=[ps], writes=[vS])
                yield
        for blk in range(4):
            qT = BFp.get()
            qT2 = BFp.get()
            qTs = [qT, qT2]
            wq = load_w(w_in[:, OFF_AQ + blk * 256:OFF_AQ + (blk + 1) * 256], 256)
            for tt in range(NT):
                ps = nextF()
                proj_T(hT, tt, wq, 256, ps)
                qk_norm_T(ps, 2, 128, [dcol(DC_GQS)],
                          lambda i, tt=tt, qTs=qTs: (qTs[i][:, tt * 128:(tt + 1) * 128], qTs[i]))
                yield
            flush()
            wz = load_w(w_in[:, OFF_AZ + blk * 256:OFF_AZ + (blk + 1) * 256], 256)
            for hh in range(2):
                h = blk * 2 + hh
                psz = nextF()
                proj_F(hT, wz, 256, hh * 128, psz)
                szT = BFp.get()
                S.op(ACT, lambda: se.activation(out=szT[:, :], in_=psz[:, :], func=AF.Silu), reads=[psz], writes=[szT])
                yield
                S.dma(SP, biasm.t[:, :], biasT_d[h], writes=[biasm])
                pv = psAcc[0]
                den = psAcc[1]
                for g in range(NT):
                    G = ps_i * NT + g
                    kts = [kt for kt in range(5) if G - 4 + kt >= 0]
                    k0 = kts[0]
                    sc = F4.get()
                    ps1 = nextF()
                    ps2 = nextF()
                    for kt in kts:
                        bt = g + kt
                        tgt = ps1[:, kt * 128:(kt + 1) * 128] if kt < 4 else ps2[:, 0:128]
                        tb = ps1 if kt < 4 else ps2
                        S.op(PE, lambda: te.matmul(tgt, lhsT=kT.t[:, h, bt * 128:(bt + 1) * 128],
                                                   rhs=qTs[hh][:, g * 128:(g + 1) * 128], start=True, stop=True),
                             reads=[kT, qTs[hh]], writes=[tb])
                    pT = BFp.get()
                    n1 = 4 - k0 if k0 < 4 else 0
                    if n1 > 0:
                        S.op(DVE, lambda: ve.tensor_tensor(out=sc[:, k0 * 128:512], in0=ps1[:, k0 * 128:512],
                                                           in1=biasm.t[:, k0 * 128:512], op=ALU.add),
                             reads=[ps1, biasm], writes=[sc])
                        S.op(ACT, lambda: se.activation(out=pT[:, k0 * 128:512], in_=sc[:, k0 * 128:512], func=AF.Exp),
                             reads=[sc], writes=[pT])
                        if k0 == 0:
                            S.op(DVE, lambda: ve.memset(pT[0:64, 64:128], 0.0), writes=[pT])
                    sc4 = F4.get()
                    pT5 = BFp.get()
                    S.op(DVE, lambda: ve.tensor_tensor(out=sc4[:, 0:128], in0=ps2[:, 0:128], in1=biasm.t[:, 512:640],
                                                       op=ALU.add), reads=[ps2, biasm], writes=[sc4])
                    S.op(ACT, lambda: se.activation(out=pT5[:, 0:128], in_=sc4[:, 0:128], func=AF.Exp),
                         reads=[sc4], writes=[pT5])
                    S.op(DVE, lambda: ve.memset(pT5[64:128, 0:64], 0.0), writes=[pT5])
                    for i, kt in enumerate(kts):
                        bt = g + kt
                        src = pT[:, kt * 128:(kt + 1) * 128] if kt < 4 else pT5[:, 0:128]
                        sb = pT if kt < 4 else pT5
                        S.op(PE, lambda: te.matmul(pv[:, g * 128:(g + 1) * 128], lhsT=vS.t[:, bt, h * 128:(h + 1) * 128],
                                                   rhs=src, start=(i == 0), stop=(i == len(kts) - 1)),
                             reads=[vS, sb], writes=[pv], inc=(i == len(kts) - 1))
                    for i, kt in enumerate(kts):
                        src = pT[:, kt * 128:(kt + 1) * 128] if kt < 4 else pT5[:, 0:128]
                        sb = pT if kt < 4 else pT5
                        S.op(PE, lambda: te.matmul(den[:, g * 128:(g + 1) * 128], lhsT=onesb.t[:, :], rhs=src,
                                                   start=(i == 0), stop=(i == len(kts) - 1)),
                             reads=[onesb, sb], writes=[den], inc=(i == len(kts) - 1))
                    F4.put(sc, sc4)
                    BFp.put(pT, pT5)
                    yield
                rden = F4.get()
                S.op(DVE, lambda: ve.reciprocal(out=rden[:, :], in_=den[:, :]), reads=[den], writes=[rden])
                S.op(DVE, lambda: ve.tensor_tensor(out=rden[:, :], in0=pv[:, :], in1=rden[:, :], op=ALU.mult),
                     reads=[pv, rden], writes=[rden])
                S.op(DVE, lambda: ve.tensor_tensor(out=yaT.t[:, h, :], in0=rden[:, :], in1=szT[:, :], op=ALU.mult),
                     reads=[rden, szT], writes=[yaT])
                F4.put(rden)
                BFp.put(szT)
            BFp.put(qT, qT2)

    def phaseC(ps_i):
        cqT = [BFp.get() for _ in range(2)]
        for hc in range(4):
            wq = load_w(w_in[:, OFF_CQ + hc * 256:OFF_CQ + (hc + 1) * 256], 256)
            for tt in range(NT):
                ps = nextF()
                proj_T(hT, tt, wq, 256, ps)
                qk_norm_T(ps, 1, 256, [dcol(DC_CQS), dcol(DC_CQS + 1)],
                          lambda i, tt=tt: (cqT[i][:, tt * 128:(tt + 1) * 128], cqT[i]))
                yield
            flush()
            pTs = []
            for mt in range(2):
                ps = nextF()
                for dc in range(2):
                    S.op(PE, lambda: te.matmul(ps[:, :], lhsT=mkT.t[:, hc * 2 + dc, mt * 128:(mt + 1) * 128],
                                               rhs=cqT[dc][:, :], start=(dc == 0), stop=(dc == 1)),
                         reads=[mkT, cqT[dc]], writes=[ps], inc=(dc == 1))
                pT = BFp.get()
                S.op(ACT, lambda: se.activation(out=pT[:, :], in_=ps[:, :], func=AF.Exp), reads=[ps], writes=[pT])
                pTs.append(pT)
                yield
            den = nextF()
            for mt in range(2):
                S.op(PE, lambda: te.matmul(den[:, :], lhsT=onesb.t[:, :], rhs=pTs[mt][:, :], start=(mt == 0), stop=(mt == 1)),
                     reads=[onesb, pTs[mt]], writes=[den], inc=(mt == 1))
            rden = F4.get()
            S.op(DVE, lambda: ve.reciprocal(out=rden[:, :], in_=den[:, :]), reads=[den], writes=[rden])
            wz = load_w(w_in[:, OFF_CZ + hc * 256:OFF_CZ + (hc + 1) * 256], 256)
            for dvc in range(2):
                pv = nextF()
                for mt in range(2):
                    S.op(PE, lambda: te.matmul(pv[:, :], lhsT=mvS.t[:, mt, hc * 256 + dvc * 128:hc * 256 + (dvc + 1) * 128],
                                               rhs=pTs[mt][:, :], start=(mt == 0), stop=(mt == 1)),
                         reads=[mvS, pTs[mt]], writes=[pv], inc=(mt == 1))
                psz = nextF()
                proj_F(hT, wz, 256, dvc * 128, psz)
                szT = BFp.get()
                S.op(ACT, lambda: se.activation(out=szT[:, :], in_=psz[:, :], func=AF.Silu), reads=[psz], writes=[szT])
                y = F4.get()
                S.op(DVE, lambda: ve.tensor_tensor(out=y[:, :], in0=pv[:, :], in1=rden[:, :], op=ALU.mult),
                     reads=[pv, rden], writes=[y])
                S.op(DVE, lambda: ve.tensor_tensor(out=ycT.t[:, hc * 2 + dvc, :], in0=y[:, :], in1=szT[:, :], op=ALU.mult),
                     reads=[y, szT], writes=[ycT])
                F4.put(y)
                BFp.put(szT)
                yield
            F4.put(rden)
            BFp.put(*pTs)
        BFp.put(*cqT)

    def phaseB(ps_i, side=None):
        w = load_w(w_in[:, OFF_WD:OFF_WD + 128], 128)
        ps = nextF()
        proj_F(hT, w, 128, 0, ps)
        pwa = F4.get(2)
        S.op(ACT, lambda: se.copy(out=pwa[:, 0:1], in_=carry_wa.t[:, 0:1]), reads=[carry_wa], writes=[pwa])
        S.op(ACT, lambda: se.copy(out=pwa[:, 1:TH + 1], in_=ps[:, :]), reads=[ps], writes=[pwa])
        S.op(ACT, lambda: se.copy(out=carry_wa.t[:, 0:1], in_=pwa[:, TH:TH + 1]), reads=[pwa], writes=[carry_wa])
        tmp = F4.get()
        S.op(ACT, lambda: se.mul(out=tmp[:, :], in_=pwa[:, 0:TH], mul=pcol(PC_MUWA)), reads=[pwa, params], writes=[tmp])
        wa = F4.get()
        S.op(DVE, lambda: ve.scalar_tensor_tensor(out=wa[:, :], in0=pwa[:, 1:TH + 1], scalar=dcol(DC_OMWA), in1=tmp[:, :],
                                                  op0=ALU.mult, op1=ALU.add), reads=[pwa, derived, tmp], writes=[wa])
        twa = BFp.get()
        S.op(ACT, lambda: se.activation(out=twa[0:64, :], in_=wa[0:64, :], func=AF.Tanh), reads=[wa], writes=[twa])
        S.op(ACT, lambda: se.copy(out=twa[64:128, :], in_=wa[64:128, :]), reads=[wa], writes=[twa])
        F4.put(pwa, tmp, wa)

        TMcs = [TMc, TMc2]

        def prep(p):
                srcrk = w_in[:, OFF_BR:OFF_BR + 2048].rearrange("(k q) (g c) -> q k g c", q=128, c=1024)[:, :, :, p * 128:(p + 1) * 128]
                srcvz = w_in[:, OFF_BV:OFF_BV + 2048].rearrange("(k q) (g c) -> q k g c", q=128, c=1024)[:, :, :, p * 128:(p + 1) * 128]
                wrk = nextW()
                wvz = nextW()
                for wq_, srcq_ in ((wrk, srcrk), (wvz, srcvz)):
                    dv_ = wq_.t[:, 0:KC * 256].rearrange("q (k g c) -> q k g c", g=2, c=128)
                    for g_ in range(2):
                        for gi in range(4):
                            S.dma(POOL, dv_[:, gi * 4:(gi + 1) * 4, g_, :], srcq_[:, gi * 4:(gi + 1) * 4, g_, :],
                                  writes=[Buf(wq_.t, [wq_.regs[g_ * 4 + gi]])])
                P0 = PC_B + p

                def bcol(q):
                    return pcol(PC_B + q * 8 + p)

                def lerp(wsrc, c0, qi, ci):
                    psx = nextF()
                    proj_F(hT, wsrc, 256, c0, psx)
                    buf = F4.get(2)
                    S.op(ACT, lambda: se.copy(out=buf[:, 0:1], in_=carry.t[:, p, ci:ci + 1]), reads=[carry], writes=[buf])
                    S.op(ACT, lambda: se.copy(out=buf[:, 1:TH + 1], in_=psx[:, :]), reads=[psx], writes=[buf])
                    S.op(ACT, lambda: se.copy(out=carry.t[:, p, ci:ci + 1], in_=buf[:, TH:TH + 1]), reads=[buf], writes=[carry])
                    tm = F4.get()
                    S.op(ACT, lambda: se.mul(out=tm[:, :], in_=buf[:, 0:TH], mul=bcol(qi)), reads=[buf, params], writes=[tm])
                    o = F4.get()
                    S.op(DVE, lambda: ve.scalar_tensor_tensor(out=o[:, :], in0=buf[:, 1:TH + 1], scalar=dcol(DC_OMM + ci * 8 + p),
                                                              in1=tm[:, :], op0=ALU.mult, op1=ALU.add),
                         reads=[buf, derived, tm], writes=[o])
                    F4.put(buf, tm)
                    return o

                r = lerp(wrk, 0, PB_MUR, 0)
                yield
                k0 = lerp(wrk, 128, PB_MUK, 1)
                yield
                v = lerp(wvz, 0, PB_MUV, 2)
                yield
                psz = nextF()
                proj_F(hT, wvz, 256, 128, psz)
                szb = BFp.get()
                S.op(ACT, lambda: se.activation(out=szb[:, :], in_=psz[:, :], func=AF.Silu), reads=[psz], writes=[szb])
                yield
                psw = nextF()
                S.op(PE, lambda: te.matmul(psw[:, :], lhsT=w2a2.t[0:64, p * 128:(p + 1) * 128], rhs=twa[0:64, :],
                                           start=True, stop=True), reads=[w2a2, twa], writes=[psw])
                psa = nextF()
                S.op(PE, lambda: te.matmul(psa[:, :], lhsT=w2a2.t[64:128, p * 128:(p + 1) * 128], rhs=twa[64:128, :],
                                           start=True, stop=True), reads=[w2a2, twa], writes=[psa])
                lw = F4.get()
                S.op(ACT, lambda: se.activation(out=lw[:, :], in_=psw[:, :], func=AF.Sigmoid, bias=bcol(PB_W0)),
                     reads=[psw, params], writes=[lw])
                S.op(DVE, lambda: ve.tensor_scalar(out=lw[:, :], in0=lw[:, :], scalar1=-math.exp(-0.5), scalar2=None,
                                                   op0=ALU.mult), reads=[lw], writes=[lw])
                aic = F4.get()
                S.op(ACT, lambda: se.activation(out=aic[:, :], in_=psa[:, :], func=AF.Sigmoid, bias=bcol(PB_A0)),
                     reads=[psa, params], writes=[aic])
                cum = F4.get()
                S.op(DVE, lambda: ve.tensor_tensor_scan(out=cum[:, :], data0=cmask.t[:, :], data1=lw[:, :], initial=0.0,
                                                        op0=ALU.mult, op1=ALU.add), reads=[cmask, lw], writes=[cum])
                yield
                sqb = BFp.get()
                S.op(ACT, lambda: se.activation(out=sqb[:, :], in_=k0[:, :], func=AF.Square, scale=bcol(PB_KK)),
                     reads=[k0, params], writes=[sqb])
                yield
                pss = nextF()
                S.op(PE, lambda: te.matmul(pss[:, :], lhsT=blk1b.t[:, :], rhs=sqb[:, :], start=True, stop=True),
                     reads=[blk1b, sqb], writes=[pss])
                sq = F4.get()
                rn = sq
                S.op(DVE, lambda: ve.tensor_scalar(out=rn[:, :], in0=pss[:, :], scalar1=1e-24, scalar2=None,
                                                   op0=ALU.max), reads=[pss], writes=[rn])
                yield
                S.op(ACT, lambda: se.activation(out=rn[:, :], in_=rn[:, :], func=AF.Ln), reads=[rn], writes=[rn])
                yield
                S.op(ACT, lambda: se.activation(out=rn[:, :], in_=rn[:, :], func=AF.Exp, scale=-0.5), reads=[rn], writes=[rn])
                yield
                kkn = F4.get()
                S.op(DVE, lambda: ve.scalar_tensor_tensor(out=kkn[:, :], in0=k0[:, :], scalar=bcol(PB_KK), in1=rn[:, :],
                                                          op0=ALU.mult, op1=ALU.mult), reads=[k0, params, rn], writes=[kkn])
                yield
                tk = rn
                S.op(DVE, lambda: ve.tensor_scalar(out=tk[:, :], in0=aic[:, :], scalar1=bcol(PB_KA), scalar2=dcol(DC_OMKA + p),
                                                   op0=ALU.mult, op1=ALU.add), reads=[aic, params, derived], writes=[tk])
                yield
                kf = k0
                S.op(DVE, lambda: ve.tensor_tensor(out=kf[:, :], in0=k0[:, :], in1=tk[:, :], op=ALU.mult),
                     reads=[k0, tk], writes=[kf])
                yield
                bb = aic
                S.op(DVE, lambda: ve.tensor_tensor(out=bb[:, :], in0=kkn[:, :], in1=aic[:, :], op=ALU.mult),
                     reads=[kkn, aic], writes=[bb])
                yield
                rk = sqb
                S.op(DVE, lambda: ve.scalar_tensor_tensor(out=rk[:, :], in0=r[:, :], scalar=bcol(PB_RK), in1=kf[:, :],
                                                          op0=ALU.mult, op1=ALU.mult), reads=[r, params, kf], writes=[rk])
                yield
                psbc = nextF()
                S.op(PE, lambda: te.matmul(psbc[:, :], lhsT=blk1b.t[:, :], rhs=rk[:, :], start=True, stop=True),
                     reads=[blk1b, rk], writes=[psbc])
                bonus = tk
                S.op(DVE, lambda: ve.tensor_tensor(out=bonus[:, :], in0=psbc[:, :], in1=v[:, :], op=ALU.mult),
                     reads=[psbc, v], writes=[bonus])
                BFp.put(sqb)
                yield
                ARs = [CPool.get(2), CPool.get(2)]
                ARv = [a_[:, :].rearrange("q (c n) -> q c n", n=128) for a_ in ARs]
                S.op(DVE, lambda: ve.memset(ARs[0][64:128, :], 0.0), writes=[ARs[0]])
                yield
                S.op(DVE, lambda: ve.memset(ARs[1][0:64, :], 0.0), writes=[ARs[1]])
                yield
                ex = F4.get()
                S.op(DVE, lambda: ve.tensor_tensor(out=ex[:, :], in0=cum[:, :], in1=lw[:, :], op=ALU.subtract),
                     reads=[cum, lw], writes=[ex])
                yield
                S.op(ACT, lambda: se.activation(out=ex[:, :], in_=ex[:, :], func=AF.Exp), reads=[ex], writes=[ex])
                yield
                for hh in range(2):
                    rw = slice(hh * 64, hh * 64 + 64)
                    S.op(DVE, lambda: ve.scalar_tensor_tensor(out=ARv[hh][rw, :, 0:64], in0=kkn[rw, :].rearrange("q (c n) -> q c n", n=64),
                                                              scalar=-1.0, in1=ex[rw, :].rearrange("q (c n) -> q c n", n=64),
                                                              op0=ALU.mult, op1=ALU.mult), reads=[kkn, ex], writes=[ARs[hh]])
                F4.put(kkn)
                ep = ex
                S.op(ACT, lambda: se.activation(out=ep[:, :], in_=cum[:, :], func=AF.Exp), reads=[cum], writes=[ep])
                yield
                for hh in range(2):
                    rw = slice(hh * 64, hh * 64 + 64)
                    S.op(DVE, lambda: ve.tensor_tensor(out=ARv[hh][rw, :, 64:128], in0=r[rw, :].rearrange("q (c n) -> q c n", n=64),
                                                       in1=ep[rw, :].rearrange("q (c n) -> q c n", n=64), op=ALU.mult),
                         reads=[r, ep], writes=[ARs[hh]])
                PCs = PCbufs[p % 2]
                S.op(DVE, lambda: ve.tensor_copy(out=PCs[:, 0:8], in_=ep[:, :].rearrange("q (c n) -> q c n", n=64)[:, :, 63]),
                     reads=[ep], writes=[PCs])
                yield
                F4.put(ep, r)
                em = lw
                S.op(ACT, lambda: se.activation(out=em[:, :], in_=cum[:, :], func=AF.Exp, scale=-1.0), reads=[cum], writes=[em])
                yield
                BT = CPool.get()
                KT = CPool.get()
                S.op(DVE, lambda: ve.tensor_tensor(out=BT[:, :], in0=bb[:, :], in1=em[:, :], op=ALU.mult),
                     reads=[bb, em], writes=[BT])
                yield
                S.op(DVE, lambda: ve.tensor_tensor(out=KT[:, :], in0=kf[:, :], in1=em[:, :], op=ALU.mult),
                     reads=[kf, em], writes=[KT])
                yield
                eh = em
                cum3 = cum[:, :].rearrange("q (c n) -> q c n", n=64)
                S.op(DVE, lambda: ve.tensor_tensor(out=eh[:, :].rearrange("q (c n) -> q c n", n=64),
                                                   in0=cum3[:, :, 63].unsqueeze(2).to_broadcast([128, NCH, 64]), in1=cum3, op=ALU.subtract),
                     reads=[cum], writes=[eh])
                yield
                S.op(ACT, lambda: se.activation(out=eh[:, :], in_=eh[:, :], func=AF.Exp), reads=[eh], writes=[eh])
                yield
                bbh = BFp.get()
                kfh = BFp.get()
                vh = BFp.get()
                S.op(DVE, lambda: ve.tensor_tensor(out=bbh[:, :], in0=bb[:, :], in1=eh[:, :], op=ALU.mult),
                     reads=[bb, eh], writes=[bbh])
                yield
                S.op(DVE, lambda: ve.tensor_tensor(out=kfh[:, :], in0=kf[:, :], in1=eh[:, :], op=ALU.mult),
                     reads=[kf, eh], writes=[kfh])
                yield
                S.op(ACT, lambda: se.copy(out=vh[:, :], in_=v[:, :]), reads=[v], writes=[vh])
                yield
                F4.put(cum)
                TMc = TMcs[p % 2]
                TMv = TMc.t
                for c in range(NCH):
                    pst = nextB()
                    for qi, src in enumerate((vh, bbh, kfh)):
                        S.op(PE, lambda: te.transpose(out=pst[0:64, qi * 128:(qi + 1) * 128], in_=src[:, c * 64:(c + 1) * 64],
                                                      identity=identb.t[:, :]), reads=[src, identb], writes=[pst], inc=(qi == 2))
                    S.op(ACT, lambda: se.copy(out=TMv[:, c, :], in_=pst[0:64, 0:384]), reads=[pst], writes=[TMc])
                BFp.put(bbh, kfh, vh)
                F4.put(lw, v, aic, k0)
                yield dict(ARs=ARs, ARv=ARv, BT=BT, KT=KT, TMc=TMc, PCs=PCs, bonus=bonus, sq=sq, szb=szb)

        def run_gen(g, n):
            for _ in range(n):
                try:
                    r_ = next(g)
                except StopIteration:
                    return None
                if isinstance(r_, dict):
                    return r_
            return None

        def chain(p, ctx):
                ARs, ARv, BT, KT, TMc, PCs, bonus, sq, szb = (ctx[k_] for k_ in ("ARs", "ARv", "BT", "KT", "TMc", "PCs", "bonus", "sq", "szb"))
                TMv = TMc.t

                def bcol(q):
                    return pcol(PC_B + q * 8 + p)
                OTv = OT.t
                def AT_gen(bt, TM, AM3):
                    Ncur, Nnxt = NB[0], NB[1]
                    for m in range(8):
                        cl, hh = m // 2, m % 2
                        c = bt * 4 + cl
                        if m % 2 == 0:
                            psA = nextF()
                            psAT = nextF()
                        o = (m % 2) * 256
                        S.op(PE, lambda: te.matmul(psA[0:64, o:o + 128], lhsT=BT[:, c * 64:(c + 1) * 64], rhs=ARv[hh][:, c, :],
                                                   start=True, stop=True), reads=[BT, ARs[hh]], writes=[psA], inc=False)
                        S.op(PE, lambda: te.matmul(psA[0:64, o + 128:o + 256], lhsT=KT[:, c * 64:(c + 1) * 64], rhs=ARv[hh][:, c, :],
                                                   start=True, stop=True), reads=[KT, ARs[hh]], writes=[psA])
                        S.op(PE, lambda: te.matmul(psAT[0:64, (m % 2) * 64:(m % 2) * 64 + 64], lhsT=ARv[hh][:, c, 0:64],
                                                   rhs=BT[:, c * 64:(c + 1) * 64], start=True, stop=True),
                             reads=[BT, ARs[hh]], writes=[psAT])
                        if m % 2 == 1:
                            amt = CPool.get()
                            S.op(DVE, lambda: ve.tensor_tensor(out=amt[0:64, :], in0=psA[0:64, :], in1=rmaskU[:, :], op=ALU.mult),
                                 reads=[psA, consts], writes=[amt])
                            a2 = amt[0:64, :].rearrange("q (m n) -> q m n", n=256)
                            S.op(ACT, lambda: se.copy(out=Ncur.t[:, m - 1:m + 1, 0:64], in_=a2[:, :, 0:64]), reads=[amt], writes=[Ncur])
                            S.op(ACT, lambda: se.copy(out=AM3.t[:, m - 1:m + 1, :], in_=a2[:, :, 64:256]), reads=[amt], writes=[AM3])
                            S.op(DVE, lambda: ve.tensor_tensor(out=Ncur.t[:, m - 1:m + 1, 64:128],
                                                               in0=psAT[0:64, 0:128].rearrange("q (m n) -> q m n", n=64),
                                                               in1=rmaskL[:, :].unsqueeze(1).to_broadcast([64, 2, 64]), op=ALU.mult),
                                 reads=[psAT, consts], writes=[Ncur])
                            S.op(DVE, lambda: ve.tensor_tensor(out=TM.t[:, m - 1:m + 1, :], in0=a2[:, :, 0:64],
                                                               in1=ident[0:64, 0:64].unsqueeze(1).to_broadcast([64, 2, 64]), op=ALU.add),
                                 reads=[amt, consts], writes=[TM])
                            CPool.put(amt)
                            yield
                    for lvl in range(5):
                        for half in range(2):
                            psq = nextF()
                            for mm in range(4):
                                m = half * 4 + mm
                                S.op(PE, lambda: te.matmul(psq[0:64, mm * 128:mm * 128 + 64], lhsT=Ncur.t[:, m, 64:128],
                                                           rhs=Ncur.t[:, m, 0:64], start=True, stop=True),
                                     reads=[Ncur], writes=[psq], inc=False)
                                S.op(PE, lambda: te.matmul(psq[0:64, mm * 128 + 64:mm * 128 + 128], lhsT=Ncur.t[:, m, 0:64],
                                                           rhs=Ncur.t[:, m, 64:128], start=True, stop=True),
                                     reads=[Ncur], writes=[psq], inc=(mm == 3))
                            S.op(ACT, lambda: se.copy(out=Nnxt.t[:, half * 4:half * 4 + 4, :],
                                                      in_=psq[0:64, :].rearrange("q (m n) -> q m n", n=128)),
                                 reads=[psq], writes=[Nnxt])
                            yield
                        pst = nextF()
                        for m in range(8):
                            S.op(PE, lambda: te.matmul(pst[0:64, m * 64:(m + 1) * 64], lhsT=Nnxt.t[:, m, 64:128], rhs=TM.t[:, m, :],
                                                       start=True, stop=True), reads=[Nnxt, TM], writes=[pst], inc=(m == 7))
                        S.op(DVE, lambda: ve.tensor_tensor(out=TM.t[:, :, :], in0=TM.t[:, :, :],
                                                           in1=pst[0:64, :].rearrange("q (m n) -> q m n", n=64), op=ALU.add),
                             reads=[TM, pst], writes=[TM])
                        Ncur, Nnxt = Nnxt, Ncur
                        yield

                def R_fn(bt, TM, AM3, pump2):
                    for cl in range(4):
                        c = bt * 4 + cl
                        psW = nextF()
                        for hh in range(2):
                            m = cl * 2 + hh
                            S.op(PE, lambda: te.matmul(psW[0:64, hh * 64:(hh + 1) * 64], lhsT=ARv[hh][:, c, 0:64], rhs=SR.t[:, p, :],
                                                       start=True, stop=False), reads=[ARs[hh], SR], writes=[psW], inc=False)
                            S.op(PE, lambda: te.matmul(psW[0:64, hh * 64:(hh + 1) * 64], lhsT=AM3.t[:, m, 64:128],
                                                       rhs=TMv[:, c, hh * 64:(hh + 1) * 64], start=False, stop=True),
                                 reads=[AM3, TMc], writes=[psW], inc=(hh == 1))
                        w1 = CPool.get()
                        S.op(ACT, lambda: se.copy(out=w1[0:64, 0:128], in_=psW[0:64, 0:128]), reads=[psW], writes=[w1])
                        pump2()
                        psU = nextF()
                        for hh in range(2):
                            m = cl * 2 + hh
                            S.op(PE, lambda: te.matmul(psU[0:64, hh * 64:(hh + 1) * 64], lhsT=TM.t[:, m, :], rhs=w1[0:64, hh * 64:(hh + 1) * 64],
                                                       start=True, stop=True), reads=[TM, w1], writes=[psU], inc=(hh == 1))
                        S.op(ACT, lambda: se.copy(out=w1[0:64, 128:256], in_=psU[0:64, 0:128]), reads=[psU], writes=[w1])
                        pump2()
                        psO = nextF()
                        psS = nextF()
                        for hh in range(2):
                            m = cl * 2 + hh
                            rows = slice(hh * 64, hh * 64 + 64)
                            S.op(PE, lambda: te.matmul(psO[0:64, hh * 64:(hh + 1) * 64], lhsT=ARv[hh][:, c, 64:128], rhs=SR.t[:, p, :],
                                                       start=True, stop=False), reads=[ARs[hh], SR], writes=[psO], inc=False)
                            S.op(PE, lambda: te.matmul(psO[0:64, hh * 64:(hh + 1) * 64], lhsT=AM3.t[:, m, 0:64],
                                                       rhs=w1[0:64, 128 + hh * 64:128 + (hh + 1) * 64], start=False, stop=False),
                                 reads=[AM3, w1], writes=[psO], inc=False)
                            S.op(PE, lambda: te.matmul(psO[0:64, hh * 64:(hh + 1) * 64], lhsT=AM3.t[:, m, 128:192],
                                                       rhs=TMv[:, c, hh * 64:(hh + 1) * 64], start=False, stop=True),
                                 reads=[AM3, TMc], writes=[psO], inc=False)
                            S.op(PE, lambda: te.matmul(psS[:, hh * 64:(hh + 1) * 64], lhsT=TMv[:, c, 128:256],
                                                       rhs=w1[0:64, 128 + hh * 64:128 + (hh + 1) * 64], start=True, stop=False),
                                 reads=[TMc, w1], writes=[psS], inc=False)
                            S.op(PE, lambda: te.matmul(psS[:, hh * 64:(hh + 1) * 64], lhsT=TMv[:, c, 256:384],
                                                       rhs=TMv[:, c, hh * 64:(hh + 1) * 64], start=False, stop=True),
                                 reads=[TMc], writes=[psS, psO], inc=(hh == 1))
                        S.op(ACT, lambda: se.copy(out=OTv[:, c, :], in_=psO[0:64, 0:128]), reads=[psO], writes=[OT])
                        for hh in range(2):
                            rows = slice(hh * 64, hh * 64 + 64)
                            S.op(DVE, lambda: ve.scalar_tensor_tensor(out=STs.t[rows, p, :], in0=STs.t[rows, p, :], scalar=PCs[rows, c:c + 1],
                                                                      in1=psS[rows, hh * 64:(hh + 1) * 64], op0=ALU.mult, op1=ALU.add),
                                 reads=[STs, PCs, psS], writes=[STs])
                        if CH_BF:
                            S.op(ACT, lambda: se.copy(out=STb.t[:, p, :], in_=STs.t[:, p, :]), reads=[STs], writes=[STb])
                        CPool.put(w1)
                        pump2()

                def GN_fn(pump2):
                    O4 = OT.t[:, :, :].rearrange("q c (h n) -> q c h n", n=64)
                    J4 = TMc.t[:, :, 0:128].rearrange("q c (h n) -> q c h n", n=64)
                    H4 = TMc.t[:, :, 256:384].rearrange("q c (h n) -> q c h n", n=64)
                    gs = gnst

                    def g3(c0):
                        return gs.t[0:64, c0:c0 + 16].rearrange("q (c h) -> q c h", h=2)

                    def gb(c0):
                        return g3(c0).unsqueeze(3).to_broadcast([64, 8, 2, 64])

                    S.op(DVE, lambda: ve.tensor_reduce(out=g3(0), in_=O4, axis=AX.X, op=ALU.add), reads=[OT], writes=[gs])
                    S.op(DVE, lambda: ve.tensor_scalar(out=gs.t[0:64, 0:16], in0=gs.t[0:64, 0:16], scalar1=1.0 / 64, scalar2=None,
                                                       op0=ALU.mult), reads=[gs], writes=[gs])
                    pump2()
                    S.op(DVE, lambda: ve.tensor_tensor(out=O4, in0=O4, in1=gb(0), op=ALU.subtract), reads=[OT, gs], writes=[OT])
                    pump2()
                    S.op(DVE, lambda: ve.tensor_tensor(out=J4, in0=O4, in1=O4, op=ALU.mult), reads=[OT], writes=[TMc])
                    pump2()
                    S.op(DVE, lambda: ve.tensor_reduce(out=g3(16), in_=J4, axis=AX.X, op=ALU.add), reads=[TMc], writes=[gs])
                    S.op(DVE, lambda: ve.tensor_scalar(out=gs.t[0:64, 16:32], in0=gs.t[0:64, 16:32], scalar1=1.0 / 64, scalar2=GN_EPS,
                                                       op0=ALU.mult, op1=ALU.add), reads=[gs], writes=[gs])
                    pump2()
                    S.op(ACT, lambda: se.activation(out=gs.t[0:64, 32:48], in_=gs.t[0:64, 16:32], func=AF.Ln), reads=[gs], writes=[gs])
                    S.op(ACT, lambda: se.activation(out=gs.t[0:64, 32:48], in_=gs.t[0:64, 32:48], func=AF.Exp, scale=-0.5),
                         reads=[gs], writes=[gs])
                    pump2()
                    S.op(DVE, lambda: ve.tensor_tensor(out=H4, in0=O4, in1=gb(32), op=ALU.mult), reads=[OT, gs], writes=[TMc])
                    pump2()
                    pso = nextB()
                    for c in range(NCH):
                        S.op(PE, lambda: te.transpose(out=pso[:, c * 64:(c + 1) * 64], in_=TMc.t[:, c, 256:384], identity=identb.t[0:64, 0:64]),
                             reads=[TMc, identb], writes=[pso], inc=(c == NCH - 1))
                    y1 = F4.get()
                    S.op(DVE, lambda: ve.tensor_scalar(out=y1[:, :], in0=pso[:, 0:TH], scalar1=bcol(PB_LNW), scalar2=bcol(PB_LNB),
                                                       op0=ALU.mult, op1=ALU.add), reads=[pso, params], writes=[y1])
                    S.op(DVE, lambda: ve.tensor_tensor(out=y1[:, :], in0=y1[:, :], in1=bonus[:, :], op=ALU.add),
                         reads=[y1, bonus], writes=[y1])
                    S.op(DVE, lambda: ve.tensor_tensor(out=ybT.t[:, p, :], in0=y1[:, :], in1=szb[:, :], op=ALU.mult),
                         reads=[y1, szb], writes=[ybT])
                    F4.put(y1, sq)
                    CPool.put(ARs[0], ARs[1], BT, KT)
                    BFp.put(szb)

                return AT_gen, R_fn, GN_fn

        state = {"ctx": None, "gen": None}

        def pump(n=2):
            if state["gen"] is not None and state["ctx"] is None:
                r_ = run_gen(state["gen"], n)
                if r_ is not None:
                    state["ctx"] = r_
            if side is not None and n < 100:
                next(side, None)

        def adv(g_, n):
            for _ in range(n):
                if next(g_, "END") == "END":
                    break

        def start_prep(p_):
            if p_ < 8:
                state["gen"] = prep(p_)
                state["ctx"] = None
            else:
                state["gen"] = None

        def finish_prep():
            while state["gen"] is not None and state["ctx"] is None:
                pump(1000)
            return state["ctx"]

        start_prep(0)
        ctx = finish_prep()
        job = chain(0, ctx)
        start_prep(1)
        for _ in job[0](0, TMs_[0], AM3s_[0]):
            pump(3)
        for p in range(8):
            AT_gen, R_fn, GN_fn = job
            g1 = AT_gen(1, TMs_[1], AM3s_[1])
            R_fn(0, TMs_[0], AM3s_[0], lambda: (adv(g1, 2), pump(3)))
            for _ in g1:
                pump(3)
            if p + 1 < 8:
                ctx_n = finish_prep()
                job_n = chain(p + 1, ctx_n)
                gn_ = job_n[0](0, TMs_[0], AM3s_[0])
            else:
                job_n = None
                gn_ = iter(())
            R_fn(1, TMs_[1], AM3s_[1], lambda: adv(gn_, 1))
            GN_fn(lambda: adv(gn_, 1))
            for _ in gn_:
                pass
            if p + 1 < 8:
                start_prep(p + 2)
                job = job_n
        BFp.put(twa)

    def phaseM(ps_i):
        ups = (w_up_a, w_up_b, w_up_c)
        ys = (yaT, ybT, ycT)
        goffs = (OFF_GA, OFF_GB, OFF_GC)
        for dp in range(KC // 2):
            macc = [None, None]
            for bi in range(3):
                wg = load_w(w_in[:, goffs[bi] + dp * 256:goffs[bi] + (dp + 1) * 256], 256)
                wu = load_w(ups[bi][:, dp * 256:(dp + 1) * 256], 256, kc=8)
                for j in range(2):
                    dt = dp * 2 + j
                    psg = nextF()
                    proj_F(hT, wg, 256, j * 128, psg)
                    sg = F4.get()
                    S.op(ACT, lambda: se.activation(out=sg[:, :], in_=psg[:, :], func=AF.Sigmoid), reads=[psg], writes=[sg])
                    psu = nextF()
                    for kc in range(8):
                        S.op(PE, lambda: te.matmul(psu[:, :], lhsT=wu.t[:, kc * 256 + j * 128:kc * 256 + (j + 1) * 128],
                                                   rhs=ys[bi].t[:, kc, :], start=(kc == 0), stop=(kc == 7)),
                             reads=[wu, ys[bi]], writes=[psu], inc=(kc == 7))
                    S.op(DVE, lambda: ve.tensor_tensor(out=sg[:, :], in0=psu[:, :], in1=sg[:, :], op=ALU.mult),
                         reads=[psu, sg], writes=[sg])
                    if bi == 0:
                        macc[j] = sg
                    elif bi == 1:
                        S.op(DVE, lambda: ve.tensor_tensor(out=macc[j][:, :], in0=macc[j][:, :], in1=sg[:, :], op=ALU.add),
                             reads=[macc[j], sg], writes=[macc[j]])
                        F4.put(sg)
                    else:
                        S.op(DVE, lambda: ve.tensor_tensor(out=mgT.t[:, dt, :], in0=macc[j][:, :], in1=sg[:, :], op=ALU.add),
                             reads=[macc[j], sg], writes=[Buf(mgT.t[:, dt, :], [mgT.regs[dt]])])
                        F4.put(sg, macc[j])

    def phaseO(ps_i):
        t0 = ps_i * TH
        for cb in range(8):
            w = load_w(w_o[:, cb * 256:(cb + 1) * 256], 256)
            for tt in range(NT):
                ps = nextF()
                proj_T(mgT, tt, w, 256, ps)
                xt = F4.get()
                S.dma(SP, xt[:, 0:256], x_d[t0 + tt * 128:t0 + (tt + 1) * 128, cb * 256:(cb + 1) * 256], writes=[xt])
                S.op(DVE, lambda: ve.tensor_tensor(out=xt[:, 0:256], in0=ps[:, 0:256], in1=xt[:, 0:256], op=ALU.add),
                     reads=[ps, xt], writes=[xt])
                S.dma(SP, out_d[t0 + tt * 128:t0 + (tt + 1) * 128, cb * 256:(cb + 1) * 256], xt[:, 0:256], reads=[xt], final=True)
                F4.put(xt)

    if "C" in phases:
        phase_mem()
    for ps_i in range(npass):
        if "0" in phases:
            phase0(ps_i)
            if ps_i == 0:
                dump("hT", hT.t[:, :, :], [128, KC, TH], [hT])
        gens = []
        if "A" in phases:
            gens.append(phaseA(ps_i))
        elif ps_i == 0:
            S.op(DVE, lambda: ve.memset(yaT.t[:, :, :], 0.0), writes=[yaT])
        if "C" in phases:
            gens.append(phaseC(ps_i))
        elif ps_i == 0:
            S.op(DVE, lambda: ve.memset(ycT.t[:, :, :], 0.0), writes=[ycT])

        def side_gen():
            for g_ in gens:
                yield from g_

        sg_ = side_gen()
        if "B" in phases:
            phaseB(ps_i, side=sg_ if INTERLEAVE_AC else None)
        elif ps_i == 0:
            S.op(DVE, lambda: ve.memset(ybT.t[:, :, :], 0.0), writes=[ybT])
        for _ in sg_:
            pass
        if ps_i == dbg.get("_pass", 0):
            dump("yaT", yaT.t[:, :, :], [128, 8, TH], [yaT])
            dump("ybT", ybT.t[:, :, :], [128, 8, TH], [ybT])
            dump("ycT", ycT.t[:, :, :], [128, 8, TH], [ycT])
        if "M" in phases:
            phaseM(ps_i)
        if "O" in phases:
            phaseO(ps_i)
    S.finish()
    info = dict(sbuf=S.sbuf_bytes, nsem=S.nsem,
                nins={e.name: e.nins for e in (PE, ACT, DVE, POOL, SP)})
    es.close()
    return nc, dbg_out, info


def _rel_index_T():
    tq = np.arange(128)[None, None, :]
    tk = np.arange(128)[:, None, None]
    kt = np.arange(5)[None, :, None]
    dist = tq - tk + 512 - 128 * kt
    return np.clip(dist, -128, 128) + 128


def _consts():
    c = np.zeros((128, 832), np.float32)
    c[:, 0:128] = np.eye(128, dtype=np.float32)
    blk = np.zeros((128, 128), np.float32)
    blk[0:64, 0:64] = 1.0
    blk[64:128, 64:128] = 1.0
    c[:, 128:256] = blk
    su = np.triu(np.ones((64, 64), np.float32), 1)
    si = np.triu(np.ones((64, 64), np.float32), 0)
    row = np.concatenate([su, si, su, si], axis=1)
    c[0:64, 256:768] = np.concatenate([row, row], axis=1)
    c[0:64, 768:832] = np.tril(np.ones((64, 64), np.float32), -1)
    return c


def _cmask():
    cm = np.ones((128, TH), np.float32)
    cm[:, 0::64] = 0.0
    return cm


def _params(inp):
    P = np.zeros((128, NPARAM), np.float32)
    P[:, PC_G:PC_G + 16] = inp["norm_g"][0].reshape(16, 128).T
    P[:, PC_MG:PC_MG + 16] = inp["mem_norm_g"][0].reshape(16, 128).T
    P[:, PC_AQG] = inp["a_q_g"][0]
    P[:, PC_AKG] = inp["a_k_g"][0]
    P[:, PC_CQG:PC_CQG + 2] = inp["c_q_g"][0].reshape(2, 128).T
    P[:, PC_CKG:PC_CKG + 2] = inp["c_k_g"][0].reshape(2, 128).T
    P[0:64, PC_MUWA] = inp["b_mu_w"][0]
    P[64:128, PC_MUWA] = inp["b_mu_a"][0]
    vecs = [inp["b_mu_rkv"][0, 0], inp["b_mu_rkv"][0, 1], inp["b_mu_rkv"][0, 2], inp["b_w0"][0], inp["b_a0"][0],
            inp["b_k_k"][0], inp["b_k_a"][0], inp["b_r_k"][0].reshape(1024), inp["b_ln_w"][0], inp["b_ln_b"][0]]
    for q, vq in enumerate(vecs):
        P[:, PC_B + q * 8:PC_B + (q + 1) * 8] = np.asarray(vq).reshape(8, 128).T
    return P


_CACHE = {}


def kernel(**inputs):
    inp = {k: np.asarray(v) for k, v in inputs.items()}
    if "nc" not in _CACHE:
        _CACHE["nc"] = build_program()
    nc, _, info = _CACHE["nc"]
    params = _params(inp)
    consts = _consts()
    w2a2 = np.ascontiguousarray(np.concatenate([inp["b_w2"][0], inp["b_a2"][0]], axis=0))
    biasT = np.ascontiguousarray(inp["a_rel_bias"][0][:, _rel_index_T()].reshape(8, 128, 640))
    shared = {
        "w_in": np.ascontiguousarray(inp["w_in"][0]),
        "w_up_a": np.ascontiguousarray(inp["w_up_a"][0]),
        "w_up_b": np.ascontiguousarray(inp["w_up_b"][0]),
        "w_up_c": np.ascontiguousarray(inp["w_up_c"][0]),
        "w_mem_kv": np.ascontiguousarray(inp["w_mem_kv"][0]),
        "w_o": np.ascontiguousarray(inp["w_o"][0]),
        "params": params, "w2a2": w2a2, "biasT": biasT, "consts": consts, "cmask": _cmask(),
    }
    in_maps = []
    for b in range(8):
        m = dict(shared)
        m["x"] = np.ascontiguousarray(inp["x"][b])
        m["mem"] = np.ascontiguousarray(inp["mem"][b])
        in_maps.append(m)
    res = run_bass_kernel_spmd(nc, in_maps, core_ids=list(range(8)))
    return np.stack([np.asarray(r["out"]) for r in res.results], axis=0).astype(np.float32)
```

```python
import contextlib
import math
import numpy as np
import concourse.bass as bass
import concourse.mybir as mybir
from concourse.bass_utils import run_bass_kernel_spmd

F32 = mybir.dt.float32
BF16 = mybir.dt.bfloat16
AF = mybir.ActivationFunctionType
ALU = mybir.AluOpType
AX = mybir.AxisListType

D = 2048
T = 2048
TH = 512
NPASS = T // TH
NT = TH // 128
KC = D // 128
NMEM = 256
IN_COLS = 16512
OFF_AQ, OFF_AK, OFF_AV, OFF_AZ = 0, 1024, 2048, 3072
OFF_BR, OFF_BK, OFF_BV, OFF_BZ = 4096, 5120, 6144, 7168
OFF_WD, OFF_AD = 8192, 8256
OFF_CQ, OFF_CZ = 8320, 9344
OFF_GA, OFF_GB, OFF_GC = 10368, 12416, 14464
NORM_EPS = 1e-6
GN_EPS = 64e-5
CH = 64
NCH = TH // CH
SEM_LIMIT = 50000
SAME_ENG_SYNC = True
SAME_ENG_WAR = True
INTERLEAVE_AC = False
CH_BF = True
WS = 4096
NWS = 4

PC_G = 0
PC_MG = 16
PC_AQG = 32
PC_AKG = 33
PC_CQG = 34
PC_CKG = 36
PC_MUWA = 38
PC_B = 40
(PB_MUR, PB_MUK, PB_MUV, PB_W0, PB_A0, PB_KK, PB_KA, PB_RK, PB_LNW, PB_LNB) = range(10)
NPARAM = PC_B + 80


class Sem:
    _n = 0

    def __init__(self, h):
        self.h = h
        Sem._n += 1
        self.id = Sem._n


class Region:
    __slots__ = ("w", "r", "dsem", "dcount")

    def __init__(self):
        self.w = None
        self.r = {}
        self.dsem = None
        self.dcount = 0


class Buf:
    def __init__(self, t, regs):
        self.t = t
        self.regs = regs

    def __getitem__(self, k):
        return self.t[k]


class Engine:
    def __init__(self, S, name, eng, counting=True):
        self.S = S
        self.name = name
        self.eng = eng
        self.counting = counting
        self.seen = {}
        self.sem = None
        self.count = 0
        self.pending = False
        self.nins = 0
        if counting:
            self.sem = S.new_sem(name)


class Sched:
    def __init__(self, nc, es):
        self.nc = nc
        self.es = es
        self.nsem = 0
        self.PE = Engine(self, "pe", nc.tensor)
        self.ACT = Engine(self, "act", nc.scalar)
        self.DVE = Engine(self, "dve", nc.vector)
        self.POOL = Engine(self, "pool", nc.gpsimd)
        self.SP = Engine(self, "sp", nc.sync, counting=False)
        self.final = []
        self.sbuf_bytes = 0

    def new_sem(self, name):
        self.nsem += 1
        return Sem(self.es.enter_context(self.nc.semaphore("s%d_%s" % (self.nsem, name))))

    def sbuf(self, name, shape, dtype, nregs=1):
        t = self.es.enter_context(self.nc.sbuf_tensor("sb_" + name, list(shape), dtype))
        n = 1
        for s in shape[1:]:
            n *= s
        self.sbuf_bytes += n * (4 if dtype == F32 else 2)
        return Buf(t, [Region() for _ in range(nregs)])

    def psum(self, name, shape, dtype):
        t = self.es.enter_context(self.nc.psum_tensor("ps_" + name, list(shape), dtype))
        return Buf(t, [Region()])

    def _waits(self, E, reads, writes):
        need = {}

        def add(tok, war=False):
            if tok is None:
                return
            sem, val, en = tok
            if en == E.name and (E.name == "pe" or not SAME_ENG_SYNC or (war and not SAME_ENG_WAR)):
                return
            if E.seen.get(sem.id, 0) >= val:
                return
            if sem.id not in need or need[sem.id][1] < val:
                need[sem.id] = (sem, val)

        for b in reads:
            for rg in b.regs:
                add(rg.w)
        for b in writes:
            for rg in b.regs:
                add(rg.w)
                for tok in rg.r.values():
                    add(tok, war=True)
        for sem, val in need.values():
            E.eng.wait_ge(sem.h, val)
            E.seen[sem.id] = val
            E.nins += 1

    def op(self, E, fn, reads=(), writes=(), inc=True):
        self._waits(E, reads, writes)
        if inc and E.count >= SEM_LIMIT and not E.pending:
            E.sem = self.new_sem(E.name)
            E.count = 0
        ins = fn()
        E.nins += 1
        val = E.count + 1
        tok = (E.sem, val, E.name)
        if inc:
            ins.then_inc(E.sem.h, 1)
            E.count = val
            E.pending = False
        else:
            E.pending = True
        for b in writes:
            for rg in b.regs:
                rg.w = tok
                rg.r = {}
        for b in reads:
            for rg in b.regs:
                if rg.w is not tok:
                    rg.r[E.name] = tok
        return ins

    def dma(self, Q, out_ap, in_ap, reads=(), writes=(), final=False):
        self._waits(Q, reads, writes)
        ins = Q.eng.dma_start(out=out_ap, in_=in_ap)
        Q.nins += 1
        rg0 = (list(writes) + list(reads))[0].regs[0]
        if rg0.dsem is None or rg0.dcount + 16 > SEM_LIMIT:
            rg0.dsem = self.new_sem("d")
            rg0.dcount = 0
        rg0.dcount += 16
        ins.then_inc(rg0.dsem.h, 16)
        tok = (rg0.dsem, rg0.dcount, "dma")
        for b in writes:
            for rg in b.regs:
                rg.w = tok
                rg.r = {}
        for b in reads:
            for rg in b.regs:
                rg.r["dma%d" % rg0.dsem.id] = tok
        if final:
            self.final.append(tok)
        return ins

    def finish(self):
        need = {}
        for sem, val, _ in self.final:
            if sem.id not in need or need[sem.id][1] < val:
                need[sem.id] = (sem, val)
        for sem, val in need.values():
            self.SP.eng.wait_ge(sem.h, val)


class Pool:
    def __init__(self, S, name, n, w, dtype):
        self.S = S
        self.n = n
        self.w = w
        self.buf = S.sbuf(name, [128, n, w], dtype, nregs=n)
        self.free = list(range(n))
        self.name = name

    def get(self, k=1):
        fs = sorted(self.free)
        for i in fs:
            if all((i + j) in self.free for j in range(k)):
                for j in range(k):
                    self.free.remove(i + j)
                regs = self.buf.regs[i:i + k]
                if k == 1:
                    b = Buf(self.buf.t[:, i, :], regs)
                else:
                    b = Buf(self.buf.t[:, i:i + k, :].rearrange("p k n -> p (k n)"), regs)
                b.slot = (i, k)
                return b
        raise RuntimeError("pool %s exhausted (free=%s, want %d)" % (self.name, self.free, k))

    def put(self, *bs):
        for b in bs:
            i, k = b.slot
            for j in range(k):
                assert (i + j) not in self.free
                self.free.append(i + j)


def build_program(dbg=None, phases="0ACBMO", npass=NPASS):
    dbg = dbg or {}
    nc = bass.Bass("TRN2", target_bir_lowering=False)
    es = contextlib.ExitStack()
    S = Sched(nc, es)
    PE, ACT, DVE, POOL, SP = S.PE, S.ACT, S.DVE, S.POOL, S.SP
    te, se, ve, ge = nc.tensor, nc.scalar, nc.vector, nc.gpsimd

    def din(name, shape, dtype=F32):
        return nc.dram_tensor(name, list(shape), dtype, kind="ExternalInput").ap()

    x_d = din("x", [T, D])
    mem_d = din("mem", [NMEM, D])
    w_in = din("w_in", [D, IN_COLS])
    w_up_a = din("w_up_a", [1024, D])
    w_up_b = din("w_up_b", [1024, D])
    w_up_c = din("w_up_c", [1024, D])
    w_memkv = din("w_mem_kv", [D, 2048])
    w_o = din("w_o", [D, D])
    params_d = din("params", [128, NPARAM])
    w2a2_d = din("w2a2", [128, 1024])
    biasT_d = din("biasT", [8, 128, 640])
    consts_d = din("consts", [128, 832])
    cmask_d = din("cmask", [128, TH])
    out_d = nc.dram_tensor("out", [T, D], F32, kind="ExternalOutput").ap()
    dbg_out = {}

    hT = S.sbuf("hT", [128, KC, TH], BF16)
    yaT = S.sbuf("yaT", [128, 8, TH], BF16)
    ybT = S.sbuf("ybT", [128, 8, TH], BF16)
    ycT = S.sbuf("ycT", [128, 8, TH], BF16)
    mgT = S.sbuf("mgT", [128, KC, TH], BF16, nregs=KC)
    wsl = [S.sbuf("ws%d" % i, [128, WS], BF16, nregs=8) for i in range(NWS)]
    kT = S.sbuf("kT", [128, 8, 2 * TH], BF16)
    vS = S.sbuf("vS", [128, 8, 1024], BF16)
    mkT = S.sbuf("mkT", [128, 8, NMEM], BF16)
    mvS = S.sbuf("mvS", [128, 2, 1024], BF16)
    params = S.sbuf("params", [128, NPARAM], F32)
    derived = S.sbuf("derived", [128, 64], F32)
    w2a2 = S.sbuf("w2a2", [128, 1024], BF16)
    consts = S.sbuf("consts", [128, 832], F32)
    identb = S.sbuf("identb", [128, 128], BF16)
    onesb = S.sbuf("onesb", [128, 128], BF16)
    blk1b = S.sbuf("blk1b", [128, 128], BF16)
    cmask = S.sbuf("cmask", [128, TH], BF16)
    STs = S.sbuf("STs", [128, 8, 64], F32)
    carry = S.sbuf("carry", [128, 8, 4], F32)
    carry_wa = S.sbuf("carry_wa", [128, 2], F32)
    biasm = S.sbuf("biasm", [128, 640], F32)
    smalls = S.sbuf("smalls", [128, 64, 8], F32, nregs=64)
    CDT = BF16 if CH_BF else F32
    F4 = Pool(S, "F4", 12 if CH_BF else 14, 512, F32)
    BFp = Pool(S, "BFp", 20 if CH_BF else 6, 512, BF16)
    CPool = BFp if CH_BF else F4
    STb = S.sbuf("STb", [128, 8, 64], CDT) if CH_BF else None
    if CH_BF:
        NB = [Buf(mgT.t[0:64, 12 + 2 * i:14 + 2 * i, :].rearrange("q a b -> q (a b)").rearrange("q (m n) -> q m n", n=128),
                  mgT.regs[12 + 2 * i:14 + 2 * i]) for i in range(2)]
    else:
        NB = [S.sbuf("NB%d" % i, [64, 8, 128], CDT) for i in range(2)]
    TM = S.sbuf("TMm", [64, 8, 64], CDT)
    AM3 = S.sbuf("AM3", [64, 8, 192], CDT)
    TMs_ = [TM, S.sbuf("TMm2", [64, 8, 64], CDT)]
    AM3s_ = [AM3, S.sbuf("AM3b", [64, 8, 192], CDT)]
    if CH_BF:
        TMc = Buf(mgT.t[0:64, 0:6, :].rearrange("q a b -> q (a b)").rearrange("q (c n) -> q c n", n=384), mgT.regs[0:6])
        TMc2 = Buf(mgT.t[0:64, 6:12, :].rearrange("q a b -> q (a b)").rearrange("q (c n) -> q c n", n=384), mgT.regs[6:12])
    else:
        TMc = S.sbuf("TMc", [64, 8, 384], CDT)
        TMc2 = S.sbuf("TMc2", [64, 8, 384], CDT)
    PCbufs = [S.sbuf("PCb%d" % i, [128, 8], F32) for i in range(2)]
    OT = S.sbuf("OTb", [64, 8, 128], F32)
    gnst = S.sbuf("gnst", [64, 64], F32)
    psF = [S.psum("psF%d" % i, [128, 512], F32) for i in range(4)]
    psAcc = [S.psum("psAcc%d" % i, [128, 512], F32) for i in range(2)]
    psB = [S.psum("psB%d" % i, [128, 1024], BF16) for i in range(2)]
    rr = {"f": 0, "b": 0, "s": 0, "w": 0}

    def nextF():
        rr["f"] = (rr["f"] + 1) % len(psF)
        return psF[rr["f"]]

    def nextB():
        rr["b"] = (rr["b"] + 1) % len(psB)
        return psB[rr["b"]]

    def small():
        rr["s"] = (rr["s"] + 1) % 64
        i = rr["s"]
        return Buf(smalls.t[:, i, :], [smalls.regs[i]])

    def nextW():
        rr["w"] = (rr["w"] + 1) % NWS
        return wsl[rr["w"]]

    ident = Buf(consts.t[:, 0:128], consts.regs)
    blk1 = Buf(consts.t[:, 128:256], consts.regs)
    rmaskU = Buf(consts.t[0:64, 256:768], consts.regs)
    rmaskL = Buf(consts.t[0:64, 768:832], consts.regs)

    def pcol(c, n=1):
        return params.t[:, c:c + n]

    def dcol(c, n=1):
        return derived.t[:, c:c + n]

    DC_GQS, DC_CQS, DC_OMWA, DC_OMM, DC_OMKA = 0, 1, 3, 4, 28

    S.dma(SP, params.t[:, :], params_d, writes=[params])
    S.dma(SP, consts.t[:, :], consts_d, writes=[consts])
    S.dma(POOL, w2a2.t[:, :], w2a2_d, writes=[w2a2])
    S.op(DVE, lambda: ve.tensor_copy(out=identb.t[:, :], in_=ident[:, :]), reads=[consts], writes=[identb])
    S.op(DVE, lambda: ve.memset(onesb.t[:, :], 1.0), writes=[onesb])
    S.op(DVE, lambda: ve.tensor_copy(out=blk1b.t[:, :], in_=blk1[:, :]), reads=[consts], writes=[blk1b])
    S.dma(POOL, cmask.t[:, :], cmask_d, writes=[cmask])
    S.op(DVE, lambda: ve.memset(STs.t[:, :, :], 0.0), writes=[STs])
    if CH_BF:
        S.op(DVE, lambda: ve.memset(STb.t[:, :, :], 0.0), writes=[STb])
    SR = STb if CH_BF else STs
    S.op(DVE, lambda: ve.memset(carry.t[:, :, :], 0.0), writes=[carry])
    S.op(DVE, lambda: ve.memset(carry_wa.t[:, :], 0.0), writes=[carry_wa])
    S.op(DVE, lambda: ve.tensor_scalar(out=dcol(DC_GQS), in0=pcol(PC_AQG), scalar1=128.0 ** -0.5, scalar2=None,
                                       op0=ALU.mult), reads=[params], writes=[derived])
    S.op(DVE, lambda: ve.tensor_scalar(out=dcol(DC_CQS, 2), in0=pcol(PC_CQG, 2), scalar1=256.0 ** -0.5, scalar2=None,
                                       op0=ALU.mult), reads=[params], writes=[derived])
    S.op(DVE, lambda: ve.tensor_scalar(out=dcol(DC_OMWA), in0=pcol(PC_MUWA), scalar1=-1.0, scalar2=1.0,
                                       op0=ALU.mult, op1=ALU.add), reads=[params], writes=[derived])
    S.op(DVE, lambda: ve.tensor_scalar(out=dcol(DC_OMM, 24), in0=pcol(PC_B + PB_MUR * 8, 24), scalar1=-1.0, scalar2=1.0,
                                       op0=ALU.mult, op1=ALU.add), reads=[params], writes=[derived])
    S.op(DVE, lambda: ve.tensor_scalar(out=dcol(DC_OMKA, 8), in0=pcol(PC_B + PB_KA * 8, 8), scalar1=-1.0, scalar2=1.0,
                                       op0=ALU.mult, op1=ALU.add), reads=[params], writes=[derived])

    def dump(name, ap, shape, reads):
        if name not in dbg:
            return
        d = nc.dram_tensor("dbg_" + name, list(shape), ap.dtype, kind="ExternalOutput").ap()
        dbg_out[name] = d
        S.dma(SP, d, ap, reads=reads, final=True)

    def load_w(src_ap, ncols, kc=KC):
        w = nextW()
        dst = w.t[:, 0:kc * ncols].rearrange("p (k n) -> p k n", n=ncols)
        srcv = src_ap.rearrange("(k p) n -> p k n", p=128)
        for gi in range(kc // 4):
            S.dma(POOL, dst[:, gi * 4:(gi + 1) * 4, :], srcv[:, gi * 4:(gi + 1) * 4, :], writes=[Buf(w.t, [w.regs[gi]])])
        return w

    def rms_to_T(src_rows_ap, gcol, dst, col0):
        xt = F4.get(4)
        hb = BFp.get(4)
        ss = small()
        S.dma(SP, xt[:, :], src_rows_ap, writes=[xt])
        S.op(ACT, lambda: se.activation(out=hb[:, :], in_=xt[:, :], func=AF.Square, accum_out=ss[:, 0:1]),
             reads=[xt], writes=[hb, ss])
        S.op(DVE, lambda: ve.tensor_scalar(out=ss[:, 1:2], in0=ss[:, 0:1], scalar1=1.0 / D, scalar2=NORM_EPS,
                                           op0=ALU.mult, op1=ALU.add), reads=[ss], writes=[ss])
        S.op(ACT, lambda: se.activation(out=ss[:, 2:3], in_=ss[:, 1:2], func=AF.Ln), reads=[ss], writes=[ss])
        S.op(ACT, lambda: se.activation(out=ss[:, 2:3], in_=ss[:, 2:3], func=AF.Exp, scale=-0.5), reads=[ss], writes=[ss])
        S.op(DVE, lambda: ve.tensor_scalar(out=hb[:, :], in0=xt[:, :], scalar1=ss[:, 2:3], scalar2=None,
                                           op0=ALU.mult), reads=[xt, ss], writes=[hb])
        for half in range(2):
            pb = nextB()
            for j in range(8):
                kc = half * 8 + j
                S.op(PE, lambda: te.transpose(out=pb[:, j * 128:(j + 1) * 128], in_=hb[:, kc * 128:(kc + 1) * 128],
                                              identity=identb.t[:, :]),
                     reads=[hb, identb], writes=[pb], inc=(j == 7))
            S.op(DVE, lambda: ve.tensor_tensor(out=dst.t[:, half * 8:(half + 1) * 8, col0:col0 + 128],
                                               in0=pb[:, 0:1024].rearrange("q (k n) -> q k n", n=128),
                                               in1=params.t[:, gcol + half * 8:gcol + half * 8 + 8].unsqueeze(2).to_broadcast([128, 8, 128]),
                                               op=ALU.mult), reads=[pb, params], writes=[dst])
        F4.put(xt)
        BFp.put(hb)

    def proj_T(src, tt, w, ncols, ps, ntok_off=0):
        for kc in range(KC):
            S.op(PE, lambda: te.matmul(ps[:, 0:ncols], lhsT=src.t[:, kc, ntok_off + tt * 128:ntok_off + (tt + 1) * 128],
                                       rhs=w.t[:, kc * ncols:(kc + 1) * ncols], start=(kc == 0), stop=(kc == KC - 1)),
                 reads=[src, w], writes=[ps], inc=(kc == KC - 1))

    def proj_F(src, w, ncols, c0, ps, ntok=TH):
        for kc in range(KC):
            S.op(PE, lambda: te.matmul(ps[:, 0:ntok], lhsT=w.t[:, kc * ncols + c0:kc * ncols + c0 + 128],
                                       rhs=src.t[:, kc, 0:ntok], start=(kc == 0), stop=(kc == KC - 1)),
                 reads=[src, w], writes=[ps], inc=(kc == KC - 1))

    deferred = []

    def flush(keep=0):
        while len(deferred) > keep:
            deferred.pop(0)()

    def qk_norm_T(ps, nh, hd, gcols, dst_fn, eps=NORM_EPS):
        ndc = hd // 128
        ss = small()
        junk = BFp.get()
        for h in range(nh):
            S.op(ACT, lambda: se.activation(out=junk[:, 0:hd], in_=ps[:, h * hd:(h + 1) * hd], func=AF.Square,
                                            accum_out=ss[:, h:h + 1]), reads=[ps], writes=[junk, ss])
        S.op(DVE, lambda: ve.tensor_scalar(out=ss[:, 4:4 + nh], in0=ss[:, 0:nh], scalar1=1.0 / hd, scalar2=eps,
                                           op0=ALU.mult, op1=ALU.add), reads=[ss], writes=[ss])
        S.op(ACT, lambda: se.activation(out=ss[:, 0:nh], in_=ss[:, 4:4 + nh], func=AF.Ln), reads=[ss], writes=[ss])
        S.op(ACT, lambda: se.activation(out=ss[:, 0:nh], in_=ss[:, 0:nh], func=AF.Exp, scale=-0.5), reads=[ss], writes=[ss])
        qn = junk
        S.op(DVE, lambda: ve.tensor_tensor(out=qn[:, 0:nh * hd].rearrange("q (h n) -> q h n", n=hd),
                                           in0=ps[:, 0:nh * hd].rearrange("q (h n) -> q h n", n=hd),
                                           in1=ss[:, 0:nh].unsqueeze(2).to_broadcast([128, nh, hd]), op=ALU.mult),
             reads=[ps, ss], writes=[qn])
        nblk = nh * ndc

        def part2():
            pb = nextB()
            for i in range(nblk):
                S.op(PE, lambda: te.transpose(out=pb[:, i * 128:(i + 1) * 128], in_=qn[:, i * 128:(i + 1) * 128],
                                              identity=identb.t[:, :]), reads=[qn, identb], writes=[pb], inc=(i == nblk - 1))
            for i in range(nblk):
                dc = i % ndc
                oap, obuf = dst_fn(i)
                S.op(ACT, lambda: se.mul(out=oap, in_=pb[:, i * 128:(i + 1) * 128], mul=gcols[dc]),
                     reads=[pb, params, derived], writes=[obuf])
            BFp.put(junk)

        deferred.append(part2)
        flush(keep=1)

    def phase_mem():
        for mt in range(2):
            rms_to_T(mem_d[mt * 128:(mt + 1) * 128, :], PC_MG, mgT, mt * 128)
        for blk in range(4):
            w = load_w(w_memkv[:, blk * 256:(blk + 1) * 256], 256)
            for mt in range(2):
                ps = nextF()
                proj_T(mgT, mt, w, 256, ps)
                qk_norm_T(ps, 1, 256, [pcol(PC_CKG), pcol(PC_CKG + 1)],
                          lambda i, blk=blk, mt=mt: (mkT.t[:, blk * 2 + i, mt * 128:(mt + 1) * 128], mkT))
        flush()
        for blk in range(4):
            w = load_w(w_memkv[:, 1024 + blk * 256:1024 + (blk + 1) * 256], 256)
            for mt in range(2):
                ps = nextF()
                proj_T(mgT, mt, w, 256, ps)
                S.op(ACT, lambda: se.copy(out=mvS.t[:, mt, blk * 256:(blk + 1) * 256], in_=ps[:, 0:256]),
                     reads=[ps], writes=[mvS])

    def phase0(ps_i):
        t0 = ps_i * TH
        for tt in range(NT):
            rms_to_T(x_d[t0 + tt * 128:t0 + (tt + 1) * 128, :], PC_G, hT, tt * 128)

    def phaseA(ps_i):
        if ps_i > 0:
            S.op(POOL, lambda: ge.tensor_copy(out=kT.t[:, :, 0:TH], in_=kT.t[:, :, TH:2 * TH]), reads=[kT], writes=[kT])
            S.op(POOL, lambda: ge.tensor_copy(out=vS.t[:, 0:4, :], in_=vS.t[:, 4:8, :]), reads=[vS], writes=[vS])
        for blk in range(4):
            w = load_w(w_in[:, OFF_AK + blk * 256:OFF_AK + (blk + 1) * 256], 256)
            for tt in range(NT):
                ps = nextF()
                proj_T(hT, tt, w, 256, ps)
                qk_norm_T(ps, 2, 128, [pcol(PC_AKG)],
                          lambda i, blk=blk, tt=tt: (kT.t[:, blk * 2 + i, TH + tt * 128:TH + (tt + 1) * 128], kT))
                yield
        flush()
        for blk in range(4):
            w = load_w(w_in[:, OFF_AV + blk * 256:OFF_AV + (blk + 1) * 256], 256)
            for tt in range(NT):
                ps = nextF()
                proj_T(hT, tt, w, 256, ps)
                S.op(ACT, lambda: se.copy(out=vS.t[:, 4 + tt, blk * 256:(blk + 1) * 256], in_=ps[:, 0:256]),
                     reads=[ps], writes=[vS])
                yield
        for blk in range(4):
            qT = BFp.get()
            qT2 = BFp.get()
            qTs = [qT, qT2]
            wq = load_w(w_in[:, OFF_AQ + blk * 256:OFF_AQ + (blk + 1) * 256], 256)
            for tt in range(NT):
                ps = nextF()
                proj_T(hT, tt, wq, 256, ps)
                qk_norm_T(ps, 2, 128, [dcol(DC_GQS)],
                          lambda i, tt=tt, qTs=qTs: (qTs[i][:, tt * 128:(tt + 1) * 128], qTs[i]))
                yield
            flush()
            wz = load_w(w_in[:, OFF_AZ + blk * 256:OFF_AZ + (blk + 1) * 256], 256)
            for hh in range(2):
                h = blk * 2 + hh
                psz = nextF()
                proj_F(hT, wz, 256, hh * 128, psz)
                szT = BFp.get()
                S.op(ACT, lambda: se.activation(out=szT[:, :], in_=psz[:, :], func=AF.Silu), reads=[psz], writes=[szT])
                yield
                S.dma(SP, biasm.t[:, :], biasT_d[h], writes=[biasm])
                S.op(DVE, lambda: ve.memset(biasm.t[0:64, 64:128], -1e30), writes=[biasm])
                S.op(DVE, lambda: ve.memset(biasm.t[64:128, 512:576], -1e30), writes=[biasm])
                pv = psAcc[0]
                den = psAcc[1]
                for g in range(NT):
                    G = ps_i * NT + g
                    kts = [kt for kt in range(5) if G - 4 + kt >= 0]
                    k0 = kts[0]
                    sc = F4.get()
                    ps1 = nextF()
                    ps2 = nextF()
                    for kt in kts:
                        bt = g + kt
                        tgt = ps1[:, kt * 128:(kt + 1) * 128] if kt < 4 else ps2[:, 0:128]
                        tb = ps1 if kt < 4 else ps2
                        S.op(PE, lambda: te.matmul(tgt, lhsT=kT.t[:, h, bt * 128:(bt + 1) * 128],
                                                   rhs=qTs[hh][:, g * 128:(g + 1) * 128], start=True, stop=True),
                             reads=[kT, qTs[hh]], writes=[tb])
                    pT = BFp.get()
                    n1 = 4 - k0 if k0 < 4 else 0
                    if n1 > 0:
                        S.op(DVE, lambda: ve.tensor_tensor(out=sc[:, k0 * 128:512], in0=ps1[:, k0 * 128:512],
                                                           in1=biasm.t[:, k0 * 128:512], op=ALU.add),
                             reads=[ps1, biasm], writes=[sc])
                        S.op(ACT, lambda: se.activation(out=pT[:, k0 * 128:512], in_=sc[:, k0 * 128:512], func=AF.Exp),
                             reads=[sc], writes=[pT])
                    sc4 = F4.get()
                    pT5 = BFp.get()
                    S.op(DVE, lambda: ve.tensor_tensor(out=sc4[:, 0:128], in0=ps2[:, 0:128], in1=biasm.t[:, 512:640],
                                                       op=ALU.add), reads=[ps2, biasm], writes=[sc4])
                    S.op(ACT, lambda: se.activation(out=pT5[:, 0:128], in_=sc4[:, 0:128], func=AF.Exp),
                         reads=[sc4], writes=[pT5])
                    for i, kt in enumerate(kts):
                        bt = g + kt
                        src = pT[:, kt * 128:(kt + 1) * 128] if kt < 4 else pT5[:, 0:128]
                        sb = pT if kt < 4 else pT5
                        S.op(PE, lambda: te.matmul(pv[:, g * 128:(g + 1) * 128], lhsT=vS.t[:, bt, h * 128:(h + 1) * 128],
                                                   rhs=src, start=(i == 0), stop=(i == len(kts) - 1)),
                             reads=[vS, sb], writes=[pv], inc=(i == len(kts) - 1))
                    for i, kt in enumerate(kts):
                        src = pT[:, kt * 128:(kt + 1) * 128] if kt < 4 else pT5[:, 0:128]
                        sb = pT if kt < 4 else pT5
                        S.op(PE, lambda: te.matmul(den[:, g * 128:(g + 1) * 128], lhsT=onesb.t[:, :], rhs=src,
                                                   start=(i == 0), stop=(i == len(kts) - 1)),
                             reads=[onesb, sb], writes=[den], inc=(i == len(kts) - 1))
                    F4.put(sc, sc4)
                    BFp.put(pT, pT5)
                    yield
                rden = F4.get()
                S.op(DVE, lambda: ve.reciprocal(out=rden[:, :], in_=den[:, :]), reads=[den], writes=[rden])
                S.op(DVE, lambda: ve.tensor_tensor(out=rden[:, :], in0=pv[:, :], in1=rden[:, :], op=ALU.mult),
                     reads=[pv, rden], writes=[rden])
                S.op(DVE, lambda: ve.tensor_tensor(out=yaT.t[:, h, :], in0=rden[:, :], in1=szT[:, :], op=ALU.mult),
                     reads=[rden, szT], writes=[yaT])
                F4.put(rden)
                BFp.put(szT)
            BFp.put(qT, qT2)

    def phaseC(ps_i):
        cqT = [BFp.get() for _ in range(2)]
        for hc in range(4):
            wq = load_w(w_in[:, OFF_CQ + hc * 256:OFF_CQ + (hc + 1) * 256], 256)
            for tt in range(NT):
                ps = nextF()
                proj_T(hT, tt, wq, 256, ps)
                qk_norm_T(ps, 1, 256, [dcol(DC_CQS), dcol(DC_CQS + 1)],
                          lambda i, tt=tt: (cqT[i][:, tt * 128:(tt + 1) * 128], cqT[i]))
                yield
            flush()
            pTs = []
            for mt in range(2):
                ps = nextF()
                for dc in range(2):
                    S.op(PE, lambda: te.matmul(ps[:, :], lhsT=mkT.t[:, hc * 2 + dc, mt * 128:(mt + 1) * 128],
                                               rhs=cqT[dc][:, :], start=(dc == 0), stop=(dc == 1)),
                         reads=[mkT, cqT[dc]], writes=[ps], inc=(dc == 1))
                pT = BFp.get()
                S.op(ACT, lambda: se.activation(out=pT[:, :], in_=ps[:, :], func=AF.Exp), reads=[ps], writes=[pT])
                pTs.append(pT)
                yield
            den = nextF()
            for mt in range(2):
                S.op(PE, lambda: te.matmul(den[:, :], lhsT=onesb.t[:, :], rhs=pTs[mt][:, :], start=(mt == 0), stop=(mt == 1)),
                     reads=[onesb, pTs[mt]], writes=[den], inc=(mt == 1))
            rden = F4.get()
            S.op(DVE, lambda: ve.reciprocal(out=rden[:, :], in_=den[:, :]), reads=[den], writes=[rden])
            wz = load_w(w_in[:, OFF_CZ + hc * 256:OFF_CZ + (hc + 1) * 256], 256)
            for dvc in range(2):
                pv = nextF()
                for mt in range(2):
                    S.op(PE, lambda: te.matmul(pv[:, :], lhsT=mvS.t[:, mt, hc * 256 + dvc * 128:hc * 256 + (dvc + 1) * 128],
                                               rhs=pTs[mt][:, :], start=(mt == 0), stop=(mt == 1)),
                         reads=[mvS, pTs[mt]], writes=[pv], inc=(mt == 1))
                psz = nextF()
                proj_F(hT, wz, 256, dvc * 128, psz)
                szT = BFp.get()
                S.op(ACT, lambda: se.activation(out=szT[:, :], in_=psz[:, :], func=AF.Silu), reads=[psz], writes=[szT])
                y = F4.get()
                S.op(DVE, lambda: ve.tensor_tensor(out=y[:, :], in0=pv[:, :], in1=rden[:, :], op=ALU.mult),
                     reads=[pv, rden], writes=[y])
                S.op(DVE, lambda: ve.tensor_tensor(out=ycT.t[:, hc * 2 + dvc, :], in0=y[:, :], in1=szT[:, :], op=ALU.mult),
                     reads=[y, szT], writes=[ycT])
                F4.put(y)
                BFp.put(szT)
                yield
            F4.put(rden)
            BFp.put(*pTs)
        BFp.put(*cqT)

    def phaseB(ps_i, side=None):
        w = load_w(w_in[:, OFF_WD:OFF_WD + 128], 128)
        ps = nextF()
        proj_F(hT, w, 128, 0, ps)
        pwa = F4.get(2)
        S.op(ACT, lambda: se.copy(out=pwa[:, 0:1], in_=carry_wa.t[:, 0:1]), reads=[carry_wa], writes=[pwa])
        S.op(ACT, lambda: se.copy(out=pwa[:, 1:TH + 1], in_=ps[:, :]), reads=[ps], writes=[pwa])
        S.op(ACT, lambda: se.copy(out=carry_wa.t[:, 0:1], in_=pwa[:, TH:TH + 1]), reads=[pwa], writes=[carry_wa])
        tmp = F4.get()
        S.op(ACT, lambda: se.mul(out=tmp[:, :], in_=pwa[:, 0:TH], mul=pcol(PC_MUWA)), reads=[pwa, params], writes=[tmp])
        wa = F4.get()
        S.op(DVE, lambda: ve.scalar_tensor_tensor(out=wa[:, :], in0=pwa[:, 1:TH + 1], scalar=dcol(DC_OMWA), in1=tmp[:, :],
                                                  op0=ALU.mult, op1=ALU.add), reads=[pwa, derived, tmp], writes=[wa])
        twa = BFp.get()
        S.op(ACT, lambda: se.activation(out=twa[0:64, :], in_=wa[0:64, :], func=AF.Tanh), reads=[wa], writes=[twa])
        S.op(ACT, lambda: se.copy(out=twa[64:128, :], in_=wa[64:128, :]), reads=[wa], writes=[twa])
        F4.put(pwa, tmp, wa)

        TMcs = [TMc, TMc2]

        def prep(p):
                srcrk = w_in[:, OFF_BR:OFF_BR + 2048].rearrange("(k q) (g c) -> q k g c", q=128, c=1024)[:, :, :, p * 128:(p + 1) * 128]
                srcvz = w_in[:, OFF_BV:OFF_BV + 2048].rearrange("(k q) (g c) -> q k g c", q=128, c=1024)[:, :, :, p * 128:(p + 1) * 128]
                wrk = nextW()
                wvz = nextW()
                for wq_, srcq_ in ((wrk, srcrk), (wvz, srcvz)):
                    dv_ = wq_.t[:, 0:KC * 256].rearrange("q (k g c) -> q k g c", g=2, c=128)
                    for g_ in range(2):
                        for gi in range(4):
                            S.dma(POOL, dv_[:, gi * 4:(gi + 1) * 4, g_, :], srcq_[:, gi * 4:(gi + 1) * 4, g_, :],
                                  writes=[Buf(wq_.t, [wq_.regs[g_ * 4 + gi]])])
                P0 = PC_B + p

                def bcol(q):
                    return pcol(PC_B + q * 8 + p)

                def lerp(wsrc, c0, qi, ci):
                    psx = nextF()
                    proj_F(hT, wsrc, 256, c0, psx)
                    buf = F4.get(2)
                    S.op(ACT, lambda: se.copy(out=buf[:, 0:1], in_=carry.t[:, p, ci:ci + 1]), reads=[carry], writes=[buf])
                    S.op(ACT, lambda: se.copy(out=buf[:, 1:TH + 1], in_=psx[:, :]), reads=[psx], writes=[buf])
                    S.op(ACT, lambda: se.copy(out=carry.t[:, p, ci:ci + 1], in_=buf[:, TH:TH + 1]), reads=[buf], writes=[carry])
                    tm = F4.get()
                    S.op(ACT, lambda: se.mul(out=tm[:, :], in_=buf[:, 0:TH], mul=bcol(qi)), reads=[buf, params], writes=[tm])
                    o = F4.get()
                    S.op(DVE, lambda: ve.scalar_tensor_tensor(out=o[:, :], in0=buf[:, 1:TH + 1], scalar=dcol(DC_OMM + ci * 8 + p),
                                                              in1=tm[:, :], op0=ALU.mult, op1=ALU.add),
                         reads=[buf, derived, tm], writes=[o])
                    F4.put(buf, tm)
                    return o

                r = lerp(wrk, 0, PB_MUR, 0)
                yield
                k0 = lerp(wrk, 128, PB_MUK, 1)
                yield
                v = lerp(wvz, 0, PB_MUV, 2)
                yield
                psz = nextF()
                proj_F(hT, wvz, 256, 128, psz)
                szb = BFp.get()
                S.op(ACT, lambda: se.activation(out=szb[:, :], in_=psz[:, :], func=AF.Silu), reads=[psz], writes=[szb])
                yield
                psw = nextF()
                S.op(PE, lambda: te.matmul(psw[:, :], lhsT=w2a2.t[0:64, p * 128:(p + 1) * 128], rhs=twa[0:64, :],
                                           start=True, stop=True), reads=[w2a2, twa], writes=[psw])
                psa = nextF()
                S.op(PE, lambda: te.matmul(psa[:, :], lhsT=w2a2.t[64:128, p * 128:(p + 1) * 128], rhs=twa[64:128, :],
                                           start=True, stop=True), reads=[w2a2, twa], writes=[psa])
                lw = F4.get()
                S.op(ACT, lambda: se.activation(out=lw[:, :], in_=psw[:, :], func=AF.Sigmoid, bias=bcol(PB_W0)),
                     reads=[psw, params], writes=[lw])
                S.op(DVE, lambda: ve.tensor_scalar(out=lw[:, :], in0=lw[:, :], scalar1=-math.exp(-0.5), scalar2=None,
                                                   op0=ALU.mult), reads=[lw], writes=[lw])
                aic = F4.get()
                S.op(ACT, lambda: se.activation(out=aic[:, :], in_=psa[:, :], func=AF.Sigmoid, bias=bcol(PB_A0)),
                     reads=[psa, params], writes=[aic])
                cum = F4.get()
                S.op(DVE, lambda: ve.tensor_tensor_scan(out=cum[:, :], data0=cmask.t[:, :], data1=lw[:, :], initial=0.0,
                                                        op0=ALU.mult, op1=ALU.add), reads=[cmask, lw], writes=[cum])
                yield
                sqb = BFp.get()
                S.op(ACT, lambda: se.activation(out=sqb[:, :], in_=k0[:, :], func=AF.Square, scale=bcol(PB_KK)),
                     reads=[k0, params], writes=[sqb])
                yield
                pss = nextF()
                S.op(PE, lambda: te.matmul(pss[:, :], lhsT=blk1b.t[:, :], rhs=sqb[:, :], start=True, stop=True),
                     reads=[blk1b, sqb], writes=[pss])
                sq = F4.get()
                rn = sq
                S.op(DVE, lambda: ve.tensor_scalar(out=rn[:, :], in0=pss[:, :], scalar1=1e-24, scalar2=None,
                                                   op0=ALU.max), reads=[pss], writes=[rn])
                yield
                S.op(ACT, lambda: se.activation(out=rn[:, :], in_=rn[:, :], func=AF.Ln), reads=[rn], writes=[rn])
                yield
                S.op(ACT, lambda: se.activation(out=rn[:, :], in_=rn[:, :], func=AF.Exp, scale=-0.5), reads=[rn], writes=[rn])
                yield
                kkn = F4.get()
                S.op(DVE, lambda: ve.scalar_tensor_tensor(out=kkn[:, :], in0=k0[:, :], scalar=bcol(PB_KK), in1=rn[:, :],
                                                          op0=ALU.mult, op1=ALU.mult), reads=[k0, params, rn], writes=[kkn])
                yield
                tk = rn
                S.op(DVE, lambda: ve.tensor_scalar(out=tk[:, :], in0=aic[:, :], scalar1=bcol(PB_KA), scalar2=dcol(DC_OMKA + p),
                                                   op0=ALU.mult, op1=ALU.add), reads=[aic, params, derived], writes=[tk])
                yield
                kf = k0
                S.op(DVE, lambda: ve.tensor_tensor(out=kf[:, :], in0=k0[:, :], in1=tk[:, :], op=ALU.mult),
                     reads=[k0, tk], writes=[kf])
                yield
                bb = aic
                S.op(DVE, lambda: ve.tensor_tensor(out=bb[:, :], in0=kkn[:, :], in1=aic[:, :], op=ALU.mult),
                     reads=[kkn, aic], writes=[bb])
                yield
                rk = sqb
                S.op(DVE, lambda: ve.scalar_tensor_tensor(out=rk[:, :], in0=r[:, :], scalar=bcol(PB_RK), in1=kf[:, :],
                                                          op0=ALU.mult, op1=ALU.mult), reads=[r, params, kf], writes=[rk])
                yield
                psbc = nextF()
                S.op(PE, lambda: te.matmul(psbc[:, :], lhsT=blk1b.t[:, :], rhs=rk[:, :], start=True, stop=True),
                     reads=[blk1b, rk], writes=[psbc])
                bonus = tk
                S.op(DVE, lambda: ve.tensor_tensor(out=bonus[:, :], in0=psbc[:, :], in1=v[:, :], op=ALU.mult),
                     reads=[psbc, v], writes=[bonus])
                BFp.put(sqb)
                yield
                ARs = [CPool.get(2), CPool.get(2)]
                ARv = [a_[:, :].rearrange("q (c n) -> q c n", n=128) for a_ in ARs]
                S.op(DVE, lambda: ve.memset(ARs[0][64:128, :], 0.0), writes=[ARs[0]])
                yield
                S.op(DVE, lambda: ve.memset(ARs[1][0:64, :], 0.0), writes=[ARs[1]])
                yield
                ex = F4.get()
                S.op(DVE, lambda: ve.tensor_tensor(out=ex[:, :], in0=cum[:, :], in1=lw[:, :], op=ALU.subtract),
                     reads=[cum, lw], writes=[ex])
                yield
                S.op(ACT, lambda: se.activation(out=ex[:, :], in_=ex[:, :], func=AF.Exp), reads=[ex], writes=[ex])
                yield
                for hh in range(2):
                    rw = slice(hh * 64, hh * 64 + 64)
                    S.op(DVE, lambda: ve.scalar_tensor_tensor(out=ARv[hh][rw, :, 0:64], in0=kkn[rw, :].rearrange("q (c n) -> q c n", n=64),
                                                              scalar=-1.0, in1=ex[rw, :].rearrange("q (c n) -> q c n", n=64),
                                                              op0=ALU.mult, op1=ALU.mult), reads=[kkn, ex], writes=[ARs[hh]])
                F4.put(kkn)
                ep = ex
                S.op(ACT, lambda: se.activation(out=ep[:, :], in_=cum[:, :], func=AF.Exp), reads=[cum], writes=[ep])
                yield
                for hh in range(2):
                    rw = slice(hh * 64, hh * 64 + 64)
                    S.op(DVE, lambda: ve.tensor_tensor(out=ARv[hh][rw, :, 64:128], in0=r[rw, :].rearrange("q (c n) -> q c n", n=64),
                                                       in1=ep[rw, :].rearrange("q (c n) -> q c n", n=64), op=ALU.mult),
                         reads=[r, ep], writes=[ARs[hh]])
                PCs = PCbufs[p % 2]
                S.op(DVE, lambda: ve.tensor_copy(out=PCs[:, 0:8], in_=ep[:, :].rearrange("q (c n) -> q c n", n=64)[:, :, 63]),
                     reads=[ep], writes=[PCs])
                yield
                F4.put(ep, r)
                em = lw
                S.op(ACT, lambda: se.activation(out=em[:, :], in_=cum[:, :], func=AF.Exp, scale=-1.0), reads=[cum], writes=[em])
                yield
                BT = CPool.get()
                KT = CPool.get()
                S.op(DVE, lambda: ve.tensor_tensor(out=BT[:, :], in0=bb[:, :], in1=em[:, :], op=ALU.mult),
                     reads=[bb, em], writes=[BT])
                yield
                S.op(DVE, lambda: ve.tensor_tensor(out=KT[:, :], in0=kf[:, :], in1=em[:, :], op=ALU.mult),
                     reads=[kf, em], writes=[KT])
                yield
                eh = em
                cum3 = cum[:, :].rearrange("q (c n) -> q c n", n=64)
                S.op(DVE, lambda: ve.tensor_tensor(out=eh[:, :].rearrange("q (c n) -> q c n", n=64),
                                                   in0=cum3[:, :, 63].unsqueeze(2).to_broadcast([128, NCH, 64]), in1=cum3, op=ALU.subtract),
                     reads=[cum], writes=[eh])
                yield
                S.op(ACT, lambda: se.activation(out=eh[:, :], in_=eh[:, :], func=AF.Exp), reads=[eh], writes=[eh])
                yield
                bbh = BFp.get()
                kfh = BFp.get()
                vh = BFp.get()
                S.op(DVE, lambda: ve.tensor_tensor(out=bbh[:, :], in0=bb[:, :], in1=eh[:, :], op=ALU.mult),
                     reads=[bb, eh], writes=[bbh])
                yield
                S.op(DVE, lambda: ve.tensor_tensor(out=kfh[:, :], in0=kf[:, :], in1=eh[:, :], op=ALU.mult),
                     reads=[kf, eh], writes=[kfh])
                yield
                S.op(ACT, lambda: se.copy(out=vh[:, :], in_=v[:, :]), reads=[v], writes=[vh])
                yield
                F4.put(cum)
                TMc = TMcs[p % 2]
                TMv = TMc.t
                for c in range(NCH):
                    pst = nextB()
                    for qi, src in enumerate((vh, bbh, kfh)):
                        S.op(PE, lambda: te.transpose(out=pst[0:64, qi * 128:(qi + 1) * 128], in_=src[:, c * 64:(c + 1) * 64],
                                                      identity=identb.t[:, :]), reads=[src, identb], writes=[pst], inc=(qi == 2))
                    S.op(ACT, lambda: se.copy(out=TMv[:, c, :], in_=pst[0:64, 0:384]), reads=[pst], writes=[TMc])
                BFp.put(bbh, kfh, vh)
                F4.put(lw, v, aic, k0)
                yield dict(ARs=ARs, ARv=ARv, BT=BT, KT=KT, TMc=TMc, PCs=PCs, bonus=bonus, sq=sq, szb=szb)

        def run_gen(g, n):
            for _ in range(n):
                try:
                    r_ = next(g)
                except StopIteration:
                    return None
                if isinstance(r_, dict):
                    return r_
            return None

        def chain(p, ctx):
                ARs, ARv, BT, KT, TMc, PCs, bonus, sq, szb = (ctx[k_] for k_ in ("ARs", "ARv", "BT", "KT", "TMc", "PCs", "bonus", "sq", "szb"))
                TMv = TMc.t

                def bcol(q):
                    return pcol(PC_B + q * 8 + p)
                OTv = OT.t
                def AT_gen(bt, TM, AM3):
                    Ncur, Nnxt = NB[0], NB[1]
                    for m in range(8):
                        cl, hh = m // 2, m % 2
                        c = bt * 4 + cl
                        if m % 2 == 0:
                            psA = nextF()
                            psAT = nextF()
                        o = (m % 2) * 256
                        S.op(PE, lambda: te.matmul(psA[0:64, o:o + 128], lhsT=BT[:, c * 64:(c + 1) * 64], rhs=ARv[hh][:, c, :],
                                                   start=True, stop=True), reads=[BT, ARs[hh]], writes=[psA], inc=False)
                        S.op(PE, lambda: te.matmul(psA[0:64, o + 128:o + 256], lhsT=KT[:, c * 64:(c + 1) * 64], rhs=ARv[hh][:, c, :],
                                                   start=True, stop=True), reads=[KT, ARs[hh]], writes=[psA])
                        S.op(PE, lambda: te.matmul(psAT[0:64, (m % 2) * 64:(m % 2) * 64 + 64], lhsT=ARv[hh][:, c, 0:64],
                                                   rhs=BT[:, c * 64:(c + 1) * 64], start=True, stop=True),
                             reads=[BT, ARs[hh]], writes=[psAT])
                        if m % 2 == 1:
                            amt = CPool.get()
                            S.op(DVE, lambda: ve.tensor_tensor(out=amt[0:64, :], in0=psA[0:64, :], in1=rmaskU[:, :], op=ALU.mult),
                                 reads=[psA, consts], writes=[amt])
                            a2 = amt[0:64, :].rearrange("q (m n) -> q m n", n=256)
                            S.op(ACT, lambda: se.copy(out=Ncur.t[:, m - 1:m + 1, 0:64], in_=a2[:, :, 0:64]), reads=[amt], writes=[Ncur])
                            S.op(ACT, lambda: se.copy(out=AM3.t[:, m - 1:m + 1, :], in_=a2[:, :, 64:256]), reads=[amt], writes=[AM3])
                            S.op(DVE, lambda: ve.tensor_tensor(out=Ncur.t[:, m - 1:m + 1, 64:128],
                                                               in0=psAT[0:64, 0:128].rearrange("q (m n) -> q m n", n=64),
                                                               in1=rmaskL[:, :].unsqueeze(1).to_broadcast([64, 2, 64]), op=ALU.mult),
                                 reads=[psAT, consts], writes=[Ncur])
                            S.op(DVE, lambda: ve.tensor_tensor(out=TM.t[:, m - 1:m + 1, :], in0=a2[:, :, 0:64],
                                                               in1=ident[0:64, 0:64].unsqueeze(1).to_broadcast([64, 2, 64]), op=ALU.add),
                                 reads=[amt, consts], writes=[TM])
                            CPool.put(amt)
                            yield
                    for lvl in range(5):
                        for half in range(2):
                            psq = nextF()
                            for mm in range(4):
                                m = half * 4 + mm
                                S.op(PE, lambda: te.matmul(psq[0:64, mm * 128:mm * 128 + 64], lhsT=Ncur.t[:, m, 64:128],
                                                           rhs=Ncur.t[:, m, 0:64], start=True, stop=True),
                                     reads=[Ncur], writes=[psq], inc=False)
                                S.op(PE, lambda: te.matmul(psq[0:64, mm * 128 + 64:mm * 128 + 128], lhsT=Ncur.t[:, m, 0:64],
                                                           rhs=Ncur.t[:, m, 64:128], start=True, stop=True),
                                     reads=[Ncur], writes=[psq], inc=(mm == 3))
                            S.op(ACT, lambda: se.copy(out=Nnxt.t[:, half * 4:half * 4 + 4, :],
                                                      in_=psq[0:64, :].rearrange("q (m n) -> q m n", n=128)),
                                 reads=[psq], writes=[Nnxt])
                            yield
                        pst = nextF()
                        for m in range(8):
                            S.op(PE, lambda: te.matmul(pst[0:64, m * 64:(m + 1) * 64], lhsT=Nnxt.t[:, m, 64:128], rhs=TM.t[:, m, :],
                                                       start=True, stop=True), reads=[Nnxt, TM], writes=[pst], inc=(m == 7))
                        S.op(DVE, lambda: ve.tensor_tensor(out=TM.t[:, :, :], in0=TM.t[:, :, :],
                                                           in1=pst[0:64, :].rearrange("q (m n) -> q m n", n=64), op=ALU.add),
                             reads=[TM, pst], writes=[TM])
                        Ncur, Nnxt = Nnxt, Ncur
                        yield

                def R_fn(bt, TM, AM3, pump2):
                    for cl in range(4):
                        c = bt * 4 + cl
                        psW = nextF()
                        for hh in range(2):
                            m = cl * 2 + hh
                            S.op(PE, lambda: te.matmul(psW[0:64, hh * 64:(hh + 1) * 64], lhsT=ARv[hh][:, c, 0:64], rhs=SR.t[:, p, :],
                                                       start=True, stop=False), reads=[ARs[hh], SR], writes=[psW], inc=False)
                            S.op(PE, lambda: te.matmul(psW[0:64, hh * 64:(hh + 1) * 64], lhsT=AM3.t[:, m, 64:128],
                                                       rhs=TMv[:, c, hh * 64:(hh + 1) * 64], start=False, stop=True),
                                 reads=[AM3, TMc], writes=[psW], inc=(hh == 1))
                        w1 = CPool.get()
                        S.op(ACT, lambda: se.copy(out=w1[0:64, 0:128], in_=psW[0:64, 0:128]), reads=[psW], writes=[w1])
                        pump2()
                        psU = nextF()
                        for hh in range(2):
                            m = cl * 2 + hh
                            S.op(PE, lambda: te.matmul(psU[0:64, hh * 64:(hh + 1) * 64], lhsT=TM.t[:, m, :], rhs=w1[0:64, hh * 64:(hh + 1) * 64],
                                                       start=True, stop=True), reads=[TM, w1], writes=[psU], inc=(hh == 1))
                        S.op(ACT, lambda: se.copy(out=w1[0:64, 128:256], in_=psU[0:64, 0:128]), reads=[psU], writes=[w1])
                        pump2()
                        psO = nextF()
                        psS = nextF()
                        for hh in range(2):
                            m = cl * 2 + hh
                            rows = slice(hh * 64, hh * 64 + 64)
                            S.op(PE, lambda: te.matmul(psO[0:64, hh * 64:(hh + 1) * 64], lhsT=ARv[hh][:, c, 64:128], rhs=SR.t[:, p, :],
                                                       start=True, stop=False), reads=[ARs[hh], SR], writes=[psO], inc=False)
                            S.op(PE, lambda: te.matmul(psO[0:64, hh * 64:(hh + 1) * 64], lhsT=AM3.t[:, m, 0:64],
                                                       rhs=w1[0:64, 128 + hh * 64:128 + (hh + 1) * 64], start=False, stop=False),
                                 reads=[AM3, w1], writes=[psO], inc=False)
                            S.op(PE, lambda: te.matmul(psO[0:64, hh * 64:(hh + 1) * 64], lhsT=AM3.t[:, m, 128:192],
                                                       rhs=TMv[:, c, hh * 64:(hh + 1) * 64], start=False, stop=True),
                                 reads=[AM3, TMc], writes=[psO], inc=False)
                            S.op(PE, lambda: te.matmul(psS[:, hh * 64:(hh + 1) * 64], lhsT=TMv[:, c, 128:256],
                                                       rhs=w1[0:64, 128 + hh * 64:128 + (hh + 1) * 64], start=True, stop=False),
                                 reads=[TMc, w1], writes=[psS], inc=False)
                            S.op(PE, lambda: te.matmul(psS[:, hh * 64:(hh + 1) * 64], lhsT=TMv[:, c, 256:384],
                                                       rhs=TMv[:, c, hh * 64:(hh + 1) * 64], start=False, stop=True),
                                 reads=[TMc], writes=[psS, psO], inc=(hh == 1))
                        S.op(ACT, lambda: se.copy(out=OTv[:, c, :], in_=psO[0:64, 0:128]), reads=[psO], writes=[OT])
                        for hh in range(2):
                            rows = slice(hh * 64, hh * 64 + 64)
                            S.op(DVE, lambda: ve.scalar_tensor_tensor(out=STs.t[rows, p, :], in0=STs.t[rows, p, :], scalar=PCs[rows, c:c + 1],
                                                                      in1=psS[rows, hh * 64:(hh + 1) * 64], op0=ALU.mult, op1=ALU.add),
                                 reads=[STs, PCs, psS], writes=[STs])
                        if CH_BF:
                            S.op(ACT, lambda: se.copy(out=STb.t[:, p, :], in_=STs.t[:, p, :]), reads=[STs], writes=[STb])
                        CPool.put(w1)
                        pump2()

                def GN_fn(pump2):
                    O4 = OT.t[:, :, :].rearrange("q c (h n) -> q c h n", n=64)
                    J4 = TMc.t[:, :, 0:128].rearrange("q c (h n) -> q c h n", n=64)
                    H4 = TMc.t[:, :, 256:384].rearrange("q c (h n) -> q c h n", n=64)
                    gs = gnst

                    def g3(c0):
                        return gs.t[0:64, c0:c0 + 16].rearrange("q (c h) -> q c h", h=2)

                    def gb(c0):
                        return g3(c0).unsqueeze(3).to_broadcast([64, 8, 2, 64])

                    S.op(DVE, lambda: ve.tensor_reduce(out=g3(0), in_=O4, axis=AX.X, op=ALU.add), reads=[OT], writes=[gs])
                    S.op(DVE, lambda: ve.tensor_scalar(out=gs.t[0:64, 0:16], in0=gs.t[0:64, 0:16], scalar1=1.0 / 64, scalar2=None,
                                                       op0=ALU.mult), reads=[gs], writes=[gs])
                    pump2()
                    S.op(DVE, lambda: ve.tensor_tensor(out=O4, in0=O4, in1=gb(0), op=ALU.subtract), reads=[OT, gs], writes=[OT])
                    pump2()
                    S.op(DVE, lambda: ve.tensor_tensor(out=J4, in0=O4, in1=O4, op=ALU.mult), reads=[OT], writes=[TMc])
                    pump2()
                    S.op(DVE, lambda: ve.tensor_reduce(out=g3(16), in_=J4, axis=AX.X, op=ALU.add), reads=[TMc], writes=[gs])
                    S.op(DVE, lambda: ve.tensor_scalar(out=gs.t[0:64, 16:32], in0=gs.t[0:64, 16:32], scalar1=1.0 / 64, scalar2=GN_EPS,
                                                       op0=ALU.mult, op1=ALU.add), reads=[gs], writes=[gs])
                    pump2()
                    S.op(ACT, lambda: se.activation(out=gs.t[0:64, 32:48], in_=gs.t[0:64, 16:32], func=AF.Ln), reads=[gs], writes=[gs])
                    S.op(ACT, lambda: se.activation(out=gs.t[0:64, 32:48], in_=gs.t[0:64, 32:48], func=AF.Exp, scale=-0.5),
                         reads=[gs], writes=[gs])
                    pump2()
                    S.op(DVE, lambda: ve.tensor_tensor(out=H4, in0=O4, in1=gb(32), op=ALU.mult), reads=[OT, gs], writes=[TMc])
                    pump2()
                    pso = nextB()
                    for c in range(NCH):
                        S.op(PE, lambda: te.transpose(out=pso[:, c * 64:(c + 1) * 64], in_=TMc.t[:, c, 256:384], identity=identb.t[0:64, 0:64]),
                             reads=[TMc, identb], writes=[pso], inc=(c == NCH - 1))
                    y1 = F4.get()
                    S.op(DVE, lambda: ve.tensor_scalar(out=y1[:, :], in0=pso[:, 0:TH], scalar1=bcol(PB_LNW), scalar2=bcol(PB_LNB),
                                                       op0=ALU.mult, op1=ALU.add), reads=[pso, params], writes=[y1])
                    S.op(DVE, lambda: ve.tensor_tensor(out=y1[:, :], in0=y1[:, :], in1=bonus[:, :], op=ALU.add),
                         reads=[y1, bonus], writes=[y1])
                    S.op(DVE, lambda: ve.tensor_tensor(out=ybT.t[:, p, :], in0=y1[:, :], in1=szb[:, :], op=ALU.mult),
                         reads=[y1, szb], writes=[ybT])
                    F4.put(y1, sq)
                    CPool.put(ARs[0], ARs[1], BT, KT)
                    BFp.put(szb)

                return AT_gen, R_fn, GN_fn

        state = {"ctx": None, "gen": None}

        def pump(n=2):
            if state["gen"] is not None and state["ctx"] is None:
                r_ = run_gen(state["gen"], n)
                if r_ is not None:
                    state["ctx"] = r_
            if side is not None and n < 100:
                next(side, None)

        def adv(g_, n):
            for _ in range(n):
                if next(g_, "END") == "END":
                    break

        def start_prep(p_):
            if p_ < 8:
                state["gen"] = prep(p_)
                state["ctx"] = None
            else:
                state["gen"] = None

        def finish_prep():
            while state["gen"] is not None and state["ctx"] is None:
                pump(1000)
            return state["ctx"]

        start_prep(0)
        ctx = finish_prep()
        job = chain(0, ctx)
        start_prep(1)
        for _ in job[0](0, TMs_[0], AM3s_[0]):
            pump(3)
        for p in range(8):
            AT_gen, R_fn, GN_fn = job
            g1 = AT_gen(1, TMs_[1], AM3s_[1])
            R_fn(0, TMs_[0], AM3s_[0], lambda: (adv(g1, 2), pump(3)))
            for _ in g1:
                pump(3)
            if p + 1 < 8:
                ctx_n = finish_prep()
                job_n = chain(p + 1, ctx_n)
                gn_ = job_n[0](0, TMs_[0], AM3s_[0])
            else:
                job_n = None
                gn_ = iter(())
            R_fn(1, TMs_[1], AM3s_[1], lambda: adv(gn_, 1))
            GN_fn(lambda: adv(gn_, 1))
            for _ in gn_:
                pass
            if p + 1 < 8:
                start_prep(p + 2)
                job = job_n
        BFp.put(twa)

    def phaseM(ps_i):
        ups = (w_up_a, w_up_b, w_up_c)
        ys = (yaT, ybT, ycT)
        goffs = (OFF_GA, OFF_GB, OFF_GC)
        for dp in range(KC // 2):
            macc = [None, None]
            for bi in range(3):
                wg = load_w(w_in[:, goffs[bi] + dp * 256:goffs[bi] + (dp + 1) * 256], 256)
                wu = load_w(ups[bi][:, dp * 256:(dp + 1) * 256], 256, kc=8)
                for j in range(2):
                    dt = dp * 2 + j
                    psg = nextF()
                    proj_F(hT, wg, 256, j * 128, psg)
                    sg = F4.get()
                    S.op(ACT, lambda: se.activation(out=sg[:, :], in_=psg[:, :], func=AF.Sigmoid), reads=[psg], writes=[sg])
                    psu = nextF()
                    for kc in range(8):
                        S.op(PE, lambda: te.matmul(psu[:, :], lhsT=wu.t[:, kc * 256 + j * 128:kc * 256 + (j + 1) * 128],
                                                   rhs=ys[bi].t[:, kc, :], start=(kc == 0), stop=(kc == 7)),
                             reads=[wu, ys[bi]], writes=[psu], inc=(kc == 7))
                    S.op(DVE, lambda: ve.tensor_tensor(out=sg[:, :], in0=psu[:, :], in1=sg[:, :], op=ALU.mult),
                         reads=[psu, sg], writes=[sg])
                    if bi == 0:
                        macc[j] = sg
                    elif bi == 1:
                        S.op(DVE, lambda: ve.tensor_tensor(out=macc[j][:, :], in0=macc[j][:, :], in1=sg[:, :], op=ALU.add),
                             reads=[macc[j], sg], writes=[macc[j]])
                        F4.put(sg)
                    else:
                        S.op(DVE, lambda: ve.tensor_tensor(out=mgT.t[:, dt, :], in0=macc[j][:, :], in1=sg[:, :], op=ALU.add),
                             reads=[macc[j], sg], writes=[Buf(mgT.t[:, dt, :], [mgT.regs[dt]])])
                        F4.put(sg, macc[j])

    def phaseO(ps_i):
        t0 = ps_i * TH
        for cb in range(8):
            w = load_w(w_o[:, cb * 256:(cb + 1) * 256], 256)
            for tt in range(NT):
                ps = nextF()
                proj_T(mgT, tt, w, 256, ps)
                xt = F4.get()
                S.dma(SP, xt[:, 0:256], x_d[t0 + tt * 128:t0 + (tt + 1) * 128, cb * 256:(cb + 1) * 256], writes=[xt])
                S.op(DVE, lambda: ve.tensor_tensor(out=xt[:, 0:256], in0=ps[:, 0:256], in1=xt[:, 0:256], op=ALU.add),
                     reads=[ps, xt], writes=[xt])
                S.dma(SP, out_d[t0 + tt * 128:t0 + (tt + 1) * 128, cb * 256:(cb + 1) * 256], xt[:, 0:256], reads=[xt], final=True)
                F4.put(xt)

    if "C" in phases:
        phase_mem()
    for ps_i in range(npass):
        if "0" in phases:
            phase0(ps_i)
            if ps_i == 0:
                dump("hT", hT.t[:, :, :], [128, KC, TH], [hT])
        gens = []
        if "A" in phases:
            gens.append(phaseA(ps_i))
        elif ps_i == 0:
            S.op(DVE, lambda: ve.memset(yaT.t[:, :, :], 0.0), writes=[yaT])
        if "C" in phases:
            gens.append(phaseC(ps_i))
        elif ps_i == 0:
            S.op(DVE, lambda: ve.memset(ycT.t[:, :, :], 0.0), writes=[ycT])

        def side_gen():
            for g_ in gens:
                yield from g_

        sg_ = side_gen()
        if "B" in phases:
            phaseB(ps_i, side=sg_ if INTERLEAVE_AC else None)
        elif ps_i == 0:
            S.op(DVE, lambda: ve.memset(ybT.t[:, :, :], 0.0), writes=[ybT])
        for _ in sg_:
            pass
        if ps_i == dbg.get("_pass", 0):
            dump("yaT", yaT.t[:, :, :], [128, 8, TH], [yaT])
            dump("ybT", ybT.t[:, :, :], [128, 8, TH], [ybT])
            dump("ycT", ycT.t[:, :, :], [128, 8, TH], [ycT])
        if "M" in phases:
            phaseM(ps_i)
        if "O" in phases:
            phaseO(ps_i)
    S.finish()
    info = dict(sbuf=S.sbuf_bytes, nsem=S.nsem,
                nins={e.name: e.nins for e in (PE, ACT, DVE, POOL, SP)})
    es.close()
    return nc, dbg_out, info


def _rel_index_T():
    tq = np.arange(128)[None, None, :]
    tk = np.arange(128)[:, None, None]
    kt = np.arange(5)[None, :, None]
    dist = tq - tk + 512 - 128 * kt
    return np.clip(dist, -128, 128) + 128


def _consts():
    c = np.zeros((128, 832), np.float32)
    c[:, 0:128] = np.eye(128, dtype=np.float32)
    blk = np.zeros((128, 128), np.float32)
    blk[0:64, 0:64] = 1.0
    blk[64:128, 64:128] = 1.0
    c[:, 128:256] = blk
    su = np.triu(np.ones((64, 64), np.float32), 1)
    si = np.triu(np.ones((64, 64), np.float32), 0)
    row = np.concatenate([su, si, su, si], axis=1)
    c[0:64, 256:768] = np.concatenate([row, row], axis=1)
    c[0:64, 768:832] = np.tril(np.ones((64, 64), np.float32), -1)
    return c


def _cmask():
    cm = np.ones((128, TH), np.float32)
    cm[:, 0::64] = 0.0
    return cm


def _params(inp):
    P = np.zeros((128, NPARAM), np.float32)
    P[:, PC_G:PC_G + 16] = inp["norm_g"][0].reshape(16, 128).T
    P[:, PC_MG:PC_MG + 16] = inp["mem_norm_g"][0].reshape(16, 128).T
    P[:, PC_AQG] = inp["a_q_g"][0]
    P[:, PC_AKG] = inp["a_k_g"][0]
    P[:, PC_CQG:PC_CQG + 2] = inp["c_q_g"][0].reshape(2, 128).T
    P[:, PC_CKG:PC_CKG + 2] = inp["c_k_g"][0].reshape(2, 128).T
    P[0:64, PC_MUWA] = inp["b_mu_w"][0]
    P[64:128, PC_MUWA] = inp["b_mu_a"][0]
    vecs = [inp["b_mu_rkv"][0, 0], inp["b_mu_rkv"][0, 1], inp["b_mu_rkv"][0, 2], inp["b_w0"][0], inp["b_a0"][0],
            inp["b_k_k"][0], inp["b_k_a"][0], inp["b_r_k"][0].reshape(1024), inp["b_ln_w"][0], inp["b_ln_b"][0]]
    for q, vq in enumerate(vecs):
        P[:, PC_B + q * 8:PC_B + (q + 1) * 8] = np.asarray(vq).reshape(8, 128).T
    return P


_CACHE = {}


def kernel(**inputs):
    inp = {k: np.asarray(v) for k, v in inputs.items()}
    if "nc" not in _CACHE:
        _CACHE["nc"] = build_program()
    nc, _, info = _CACHE["nc"]
    params = _params(inp)
    consts = _consts()
    w2a2 = np.ascontiguousarray(np.concatenate([inp["b_w2"][0], inp["b_a2"][0]], axis=0))
    biasT = np.ascontiguousarray(inp["a_rel_bias"][0][:, _rel_index_T()].reshape(8, 128, 640))
    shared = {
        "w_in": np.ascontiguousarray(inp["w_in"][0]),
        "w_up_a": np.ascontiguousarray(inp["w_up_a"][0]),
        "w_up_b": np.ascontiguousarray(inp["w_up_b"][0]),
        "w_up_c": np.ascontiguousarray(inp["w_up_c"][0]),
        "w_mem_kv": np.ascontiguousarray(inp["w_mem_kv"][0]),
        "w_o": np.ascontiguousarray(inp["w_o"][0]),
        "params": params, "w2a2": w2a2, "biasT": biasT, "consts": consts, "cmask": _cmask(),
    }
    in_maps = []
    for b in range(8):
        m = dict(shared)
        m["x"] = np.ascontiguousarray(inp["x"][b])
        m["mem"] = np.ascontiguousarray(inp["mem"][b])
        in_maps.append(m)
    res = run_bass_kernel_spmd(nc, in_maps, core_ids=list(range(8)))
    return np.stack([np.asarray(r["out"]) for r in res.results], axis=0).astype(np.float32)
```
